# Optimizing a Trainium2 kernel written in Bass

```python
import jax
import jax.numpy as jnp
from jax import lax
import numpy as np

D_MODEL = 1024
BATCH = 8
SEQ = 8192
DEPTH = 4
DEC_BATCH = 4
DEC_SEQ = 4096
PAST_LEN = 128

HEAD_DIM = 128
N_HEADS = 8
N_KV_HEADS = 2
GROUP = N_HEADS // N_KV_HEADS
ATTN_DIM = N_HEADS * HEAD_DIM
KV_DIM = N_KV_HEADS * HEAD_DIM
WINDOW = 128
BLOCK = 128
ROPE_THETA = 10000.0
D_RNN = 3 * D_MODEL // 2
N_RNN_BLOCKS = 12
RNN_BLOCK = D_RNN // N_RNN_BLOCKS
CONV_WIDTH = 4
CONV_LEFT = 2
RGLRU_C = 8.0
EPS = 1e-6
D_IN = 2 * ATTN_DIM + 2 * KV_DIM + 2 * D_RNN

kernel_name = 'griffin_bidir_hybrid_encoder'


def rms_norm(x, gain):
    xf = x.astype(jnp.float32)
    y = xf * lax.rsqrt(jnp.mean(xf * xf, axis=-1, keepdims=True) + EPS)
    return (y * gain.astype(jnp.float32)).astype(x.dtype)


def rotary(x, pos):
    inv_freq = ROPE_THETA ** (-jnp.arange(0, HEAD_DIM, 2, dtype=jnp.float32) / HEAD_DIM)
    ang = pos[:, None] * inv_freq[None, :]
    cos = jnp.cos(ang)[None, :, None, :]
    sin = jnp.sin(ang)[None, :, None, :]
    xf = x.astype(jnp.float32)
    x1, x2 = jnp.split(xf, 2, axis=-1)
    out = jnp.concatenate([x1 * cos - x2 * sin, x2 * cos + x1 * sin], axis=-1)
    return out.astype(x.dtype)


def window_attention(q, k, v, sink):
    B, S = q.shape[0], q.shape[1]
    nb = S // BLOCK
    qb = q.reshape(B, nb, BLOCK, N_KV_HEADS, GROUP, HEAD_DIM)

    def band(t):
        tp = jnp.pad(t, ((0, 0), (BLOCK, BLOCK), (0, 0), (0, 0)))
        views = [tp[:, i * BLOCK:i * BLOCK + S].reshape(B, nb, BLOCK, N_KV_HEADS, HEAD_DIM)
                 for i in range(3)]
        return jnp.concatenate(views, axis=2)

    kb = band(k)
    vb = band(v)
    scores = jnp.einsum('bnqhgd,bnjhd->bnhgqj', qb, kb,
                        preferred_element_type=jnp.float32) * (HEAD_DIM ** -0.5)
    n_idx = jnp.arange(nb)[:, None, None]
    q_pos = n_idx * BLOCK + jnp.arange(BLOCK)[None, :, None]
    k_pos = (n_idx - 1) * BLOCK + jnp.arange(3 * BLOCK)[None, None, :]
    valid = (jnp.abs(k_pos - q_pos) <= WINDOW) & (k_pos >= 0) & (k_pos < S)
    scores = jnp.where(valid[None, :, None, None], scores, -1e30)
    sink_l = sink.astype(jnp.float32).reshape(N_KV_HEADS, GROUP)[None, None, :, :, None, None]
    m = jnp.maximum(jnp.max(scores, axis=-1, keepdims=True), sink_l)
    p = jnp.exp(scores - m)
    denom = jnp.sum(p, axis=-1, keepdims=True) + jnp.exp(sink_l - m)
    out = jnp.einsum('bnhgqj,bnjhd->bnqhgd', (p / denom).astype(v.dtype), vb)
    return out.reshape(B, S, ATTN_DIM)


def centred_depthwise_conv(x, w, b):
    S = x.shape[1]
    xp = jnp.pad(x, ((0, 0), (CONV_LEFT, CONV_WIDTH - 1 - CONV_LEFT), (0, 0)))
    y = xp[:, 0:S] * w[0]
    for t in range(1, CONV_WIDTH):
        y = y + xp[:, t:t + S] * w[t]
    return y + b


def _linear_recurrence_combine(left, right):
    a1, b1 = left
    a2, b2 = right
    return a1 * a2, a2 * b1 + b2


def rglru(xc, w_a, b_a, w_x, b_x, lam, reverse):
    B, S = xc.shape[0], xc.shape[1]
    xb = xc.reshape(B, S, N_RNN_BLOCKS, RNN_BLOCK)
    gate_a = jnp.einsum('bshi,hij->bshj', xb, w_a).reshape(B, S, D_RNN) + b_a
    gate_x = jnp.einsum('bshi,hij->bshj', xb, w_x).reshape(B, S, D_RNN) + b_x
    r = jax.nn.sigmoid(gate_a.astype(jnp.float32))
    i = jax.nn.sigmoid(gate_x.astype(jnp.float32))
    log_a = -RGLRU_C * jax.nn.softplus(-lam.astype(jnp.float32)) * r
    a = jnp.exp(log_a)
    b = jnp.sqrt(-jnp.expm1(2.0 * log_a)) * (i * xc.astype(jnp.float32))
    _, h = lax.associative_scan(_linear_recurrence_combine, (a, b), reverse=reverse, axis=1)
    return h


def hybrid_layer(x, cond, pos, norm_gain, w_ada, b_ada, w_in, attn_sink, conv_w, conv_b,
                 rg_w_a, rg_b_a, rg_w_x, rg_b_x, rg_lambda, w_attn_proj, w_rnn_proj,
                 w_merge, b_merge, w_out):
    B, S = x.shape[0], x.shape[1]
    mod = jax.nn.silu(cond) @ w_ada + b_ada
    shift, scale, gate = jnp.split(mod, 3, axis=-1)
    h = rms_norm(x, norm_gain) * (1.0 + scale[:, None, :]) + shift[:, None, :]
    proj = h @ w_in
    offs = [ATTN_DIM, ATTN_DIM + KV_DIM, ATTN_DIM + 2 * KV_DIM,
            2 * ATTN_DIM + 2 * KV_DIM, 2 * ATTN_DIM + 2 * KV_DIM + D_RNN]
    q, k, v, attn_g, rnn_x, rnn_g = jnp.split(proj, offs, axis=-1)
    q = rotary(q.reshape(B, S, N_HEADS, HEAD_DIM), pos)
    k = rotary(k.reshape(B, S, N_KV_HEADS, HEAD_DIM), pos)
    v = v.reshape(B, S, N_KV_HEADS, HEAD_DIM)
    attn = window_attention(q, k, v, attn_sink) * jax.nn.silu(attn_g)
    xc = centred_depthwise_conv(rnn_x, conv_w, conv_b)
    h_fwd = rglru(xc, rg_w_a[0], rg_b_a[0], rg_w_x[0], rg_b_x[0], rg_lambda[0], False)
    h_bwd = rglru(xc, rg_w_a[1], rg_b_a[1], rg_w_x[1], rg_b_x[1], rg_lambda[1], True)
    rnn = (h_fwd + h_bwd).astype(x.dtype) * jax.nn.silu(rnn_g)
    merge = jax.nn.sigmoid((h @ w_merge + b_merge).astype(jnp.float32)).astype(x.dtype)
    g_attn, g_rnn = jnp.split(merge, 2, axis=-1)
    mixed = g_attn * (attn @ w_attn_proj) + g_rnn * (rnn @ w_rnn_proj)
    return x + gate[:, None, :] * (mixed @ w_out)


def encoder_trunk(x, cond, norm_gain, w_ada, b_ada, w_in, attn_sink, conv_w, conv_b,
                  rg_w_a, rg_b_a, rg_w_x, rg_b_x, rg_lambda, w_attn_proj, w_rnn_proj,
                  w_merge, b_merge, w_out, final_gain):
    pos = jnp.arange(x.shape[1], dtype=jnp.float32)
    for l in range(DEPTH):
        x = hybrid_layer(x, cond, pos, norm_gain[l], w_ada[l], b_ada[l], w_in[l], attn_sink[l],
                         conv_w[l], conv_b[l], rg_w_a[l], rg_b_a[l], rg_w_x[l], rg_b_x[l],
                         rg_lambda[l], w_attn_proj[l], w_rnn_proj[l], w_merge[l], b_merge[l],
                         w_out[l])
    return rms_norm(x, final_gain)


def setup_inputs(seed: int = 0) -> dict:
    key = jax.random.key(seed)
    ks = jax.random.split(key, 24)
    f32 = jnp.float32

    def nrm(k, shape, s):
        return jax.random.normal(k, shape, f32) * s

    a_init = jax.random.uniform(ks[14], (DEPTH, 2, D_RNN), f32, 0.9, 0.999)
    return {
        'x_prompt': nrm(ks[0], (BATCH, SEQ, D_MODEL), 1.0),
        'x_sample': nrm(ks[1], (DEC_BATCH, DEC_SEQ, D_MODEL), 1.0),
        'c_prompt': nrm(ks[2], (BATCH, D_MODEL), 1.0),
        'c_sample': nrm(ks[3], (DEC_BATCH, D_MODEL), 1.0),
        'norm_gain': 1.0 + nrm(ks[4], (DEPTH, D_MODEL), 0.05),
        'w_ada': nrm(ks[5], (DEPTH, D_MODEL, 3 * D_MODEL), 0.5 * D_MODEL ** -0.5),
        'b_ada': nrm(ks[6], (DEPTH, 3 * D_MODEL), 0.01),
        'w_in': nrm(ks[7], (DEPTH, D_MODEL, D_IN), D_MODEL ** -0.5),
        'attn_sink': nrm(ks[8], (DEPTH, N_HEADS), 1.0),
        'conv_w': nrm(ks[9], (DEPTH, CONV_WIDTH, D_RNN), CONV_WIDTH ** -0.5),
        'conv_b': nrm(ks[10], (DEPTH, D_RNN), 0.01),
        'rg_w_a': nrm(ks[11], (DEPTH, 2, N_RNN_BLOCKS, RNN_BLOCK, RNN_BLOCK), RNN_BLOCK ** -0.5),
        'rg_b_a': nrm(ks[12], (DEPTH, 2, D_RNN), 0.01),
        'rg_w_x': nrm(ks[13], (DEPTH, 2, N_RNN_BLOCKS, RNN_BLOCK, RNN_BLOCK), RNN_BLOCK ** -0.5),
        'rg_b_x': nrm(ks[15], (DEPTH, 2, D_RNN), 0.01),
        'rg_lambda': jnp.log(a_init) - jnp.log1p(-a_init),
        'w_attn_proj': nrm(ks[16], (DEPTH, ATTN_DIM, D_MODEL), ATTN_DIM ** -0.5),
        'w_rnn_proj': nrm(ks[17], (DEPTH, D_RNN, D_MODEL), D_RNN ** -0.5),
        'w_merge': nrm(ks[18], (DEPTH, D_MODEL, 2 * D_MODEL), D_MODEL ** -0.5),
        'b_merge': nrm(ks[19], (DEPTH, 2 * D_MODEL), 0.01),
        'w_out': nrm(ks[20], (DEPTH, D_MODEL, D_MODEL), D_MODEL ** -0.5),
        'final_gain': 1.0 + nrm(ks[21], (D_MODEL,), 0.05),
    }


def reference(x_prompt, x_sample, c_prompt, c_sample, norm_gain, w_ada, b_ada, w_in, attn_sink,
              conv_w, conv_b, rg_w_a, rg_b_a, rg_w_x, rg_b_x, rg_lambda, w_attn_proj,
              w_rnn_proj, w_merge, b_merge, w_out, final_gain):
    y_prompt = encoder_trunk(x_prompt, c_prompt, norm_gain, w_ada, b_ada, w_in, attn_sink,
                             conv_w, conv_b, rg_w_a, rg_b_a, rg_w_x, rg_b_x, rg_lambda,
                             w_attn_proj, w_rnn_proj, w_merge, b_merge, w_out, final_gain)
    y_sample = encoder_trunk(x_sample, c_sample, norm_gain, w_ada, b_ada, w_in, attn_sink,
                             conv_w, conv_b, rg_w_a, rg_b_a, rg_w_x, rg_b_x, rg_lambda,
                             w_attn_proj, w_rnn_proj, w_merge, b_merge, w_out, final_gain)
    return (y_prompt, y_sample)
```

```python
import math
import os
import numpy as np
import concourse.bass as bass
import concourse.mybir as mybir
from concourse.bass_utils import run_bass_kernel_spmd

F32 = mybir.dt.float32
BF = mybir.dt.bfloat16
AF = mybir.ActivationFunctionType
ALU = mybir.AluOpType

D = 1024
KC = 8
DIN = 5632
NH = 8
NKV = 2
DR = 1536
NG = 12
T = 512
NB = 4
EPS = 1e-6
O_Q, O_K, O_V, O_AG, O_RX, O_RG = 0, 1024, 1280, 1536, 2560, 4096
PEN = float(os.environ.get("K_PEN", "1300"))
F1 = float(os.environ.get("K_F1", "0.015"))
F2 = float(os.environ.get("K_F2", "0.03"))
TWO_PI = 2.0 * math.pi
C1 = 6.28125
C2 = TWO_PI - C1


class Sched:
    ENG = ("pe", "act", "dve", "pool", "sp")
    DEFCOST = {"pe": 2150.0, "act": 660.0, "dve": 720.0, "pool": 1350.0, "sp": 350.0}

    def __init__(self, nc):
        self.nc = nc
        self.ops = []
        self.buf = {}
        self.reorder = True

    def _add(self, eng, prod, fn, reads, writes, inc, cost, lat, tab):
        idx = len(self.ops)
        sem_preds = set()
        ord_preds = set()
        for k in reads:
            st = self.buf.get(k)
            if st is not None and st[0] is not None:
                sem_preds.add(st[0])
        for k in writes:
            st = self.buf.get(k)
            if st is not None:
                if st[0] is not None:
                    sem_preds.add(st[0])
                for r in st[1]:
                    if r[0] == prod and not isinstance(prod, tuple):
                        ord_preds.add(r)
                    else:
                        sem_preds.add(r)
        if prod == "pe":
            pe_only = set(p for p in sem_preds if p[0] == "pe")
            sem_preds -= pe_only
            ord_preds |= pe_only
        if cost is None:
            cost = self.DEFCOST[eng]
        import sys as _sys
        fr = _sys._getframe(2)
        while fr.f_code.co_name in ("op", "dma", "flush", "defer"):
            fr = fr.f_back
        self.ops[-1:] = self.ops[-1:]
        self._tag = fr.f_lineno
        self.ops.append(dict(tag=self._tag, eng=eng, prod=prod, fn=fn, sem=sem_preds, ordp=ord_preds, inc=inc, sig=False,
                             cost=cost, lat=(cost if lat is None else lat), tab=tab))
        for k in reads:
            st = self.buf.setdefault(k, [None, []])
            st[1].append((prod, idx))
        for k in writes:
            self.buf[k] = [(prod, idx), []]
        return idx

    def op(self, eng, fn, reads=(), writes=(), cost=None, tab=None):
        if ":" in eng:
            eng, tab = eng.split(":")
        return self._add(eng, eng, fn, list(reads), list(writes), 1, cost, None, tab)

    def dma(self, fn, reads=(), writes=(), key=None, eng="sp", nbytes=262144):
        lat = 2300.0 + nbytes / 120.0
        return self._add(eng, ("dma", key), fn, list(reads), list(writes), 16, 350.0, lat, None)

    def _schedule(self):
        import heapq
        ops = self.ops
        n = len(ops)
        succs = [[] for _ in range(n)]
        npred = [0] * n
        for i, o in enumerate(ops):
            ps = set(p[1] for p in o["sem"]) | set(p[1] for p in o["ordp"])
            o["allp"] = ps
            npred[i] = len(ps)
            for p in ps:
                succs[p].append(i)
        ready_t = [0.0] * n
        finish = [0.0] * n
        blev = [0.0] * n
        for i in range(n - 1, -1, -1):
            m_ = 0.0
            for s_ in succs[i]:
                if blev[s_] > m_:
                    m_ = blev[s_]
            blev[i] = m_ + ops[i]["lat"]
        PRI = os.environ.get("K_PRI", "blev")
        SLACK = float(os.environ.get("K_SLACK", "0"))
        ALPHA = float(os.environ.get("K_ALPHA", "0.001"))
        heaps = {e: [] for e in self.ENG}
        tfree = {e: 0.0 for e in self.ENG}
        curtab = [None]
        for i in range(n):
            if npred[i] == 0:
                heapq.heappush(heaps[ops[i]["eng"]], (0.0, i))
        order = []
        WIN = int(os.environ.get("K_WIN", "24"))
        done = 0
        while done < n:
            best = None
            for e in self.ENG:
                h = heaps[e]
                if not h:
                    continue
                cand = heapq.nsmallest(WIN, h)
                tf = tfree[e]
                pick = None
                for rt, i in cand:
                    stt = max(rt, tf)
                    pen = 0.0
                    if e == "act":
                        tb = ops[i]["tab"]
                        if tb is not None and tb != curtab[0]:
                            pen = PEN
                    if PRI == "hyb":
                        key = (stt + pen - ALPHA * blev[i], i)
                    elif PRI == "blev":
                        key = (round((stt + pen) / max(SLACK, 1.0)) if SLACK > 0 else stt + pen, -blev[i], i)
                    else:
                        key = (round((stt + pen) / max(SLACK, 1.0)) if SLACK > 0 else stt + pen, i)
                    if pick is None or key < pick[0]:
                        pick = (key, rt, i, stt + pen)
                if best is None or (pick[3], pick[2]) < (best[0][3], best[0][2]):
                    best = (pick, e)
            (key, rt, i, stt), e = best
            heaps[e].remove((rt, i))
            heapq.heapify(heaps[e])
            o = ops[i]
            if e == "act" and o["tab"] is not None:
                if curtab[0] != o["tab"]:
                    self.nswitch = getattr(self, "nswitch", 0) + 1
                curtab[0] = o["tab"]
            if getattr(self, "diag", None) is not None and stt > tfree[e] + 1.0:
                bp = max(o["allp"], key=lambda p_: finish[p_]) if o["allp"] else None
                kk = (e, o["tag"], ops[bp]["eng"] if bp is not None else None, ops[bp]["tag"] if bp is not None else None)
                self.diag[kk] = self.diag.get(kk, 0.0) + (stt - tfree[e])
            tfree[e] = stt + o["cost"]
            finish[i] = stt + o["lat"]
            order.append(i)
            done += 1
            for s_ in succs[i]:
                npred[s_] -= 1
                if finish[i] > ready_t[s_]:
                    ready_t[s_] = finish[i]
                if npred[s_] == 0:
                    heapq.heappush(heaps[ops[s_]["eng"]], (ready_t[s_], s_))
        self.sim_time = max(finish) if finish else 0.0
        return order

    def emit(self):
        nc = self.nc
        ops = self.ops
        order = self._schedule() if self.reorder else list(range(len(ops)))
        for o in ops:
            for pp, pi in o["sem"]:
                ops[pi]["sig"] = True
        sems, cnt = {}, {}
        val = [0] * len(ops)
        for i in order:
            o = ops[i]
            p = o["prod"]
            if isinstance(p, tuple):
                o["sig"] = True
            if o["sig"]:
                if p not in sems:
                    sems[p] = nc.alloc_semaphore("s%d" % len(sems))
                    cnt[p] = 0
                cnt[p] += o["inc"]
                val[i] = cnt[p]
        plist = list(sems.keys())
        pidx = {p_: j for j, p_ in enumerate(plist)}
        know_eng = {e: np.zeros(len(plist), np.int64) for e in self.ENG}
        know_op = {}
        prog = {e: [] for e in self.ENG}
        self.n_waits = 0
        for i in order:
            o = ops[i]
            e = o["eng"]
            K = know_eng[e]
            need = {}
            for pp, pi in o["sem"]:
                v = val[pi]
                assert v > 0
                if pp not in need or need[pp][0] < v:
                    need[pp] = (v, pi)
            waits = []
            for pp, (v, pi) in sorted(need.items(), key=lambda kv: -kv[1][0]):
                if K[pidx[pp]] < v:
                    waits.append((sems[pp], v))
                    np.maximum(K, know_op[pi], out=K)
            self.n_waits += len(waits)
            if o["sig"]:
                snap = K.copy()
                snap[pidx[o["prod"]]] = val[i]
                know_op[i] = snap
            prog[e].append((waits, o["fn"], (sems[o["prod"]], o["inc"]) if o["sig"] else None))
        final_waits = [(sems[p], cnt[p]) for p in sems if isinstance(p, tuple)]
        self.n_sems = len(sems)
        self.max_val = max(cnt.values()) if cnt else 0

        with nc.Block() as block:
            def run(e, name):
                for waits, fn, inc in prog[name]:
                    for s, v in waits:
                        e.wait_ge(s, v)
                    ins = fn(e)
                    if inc is not None:
                        ins.then_inc(inc[0], inc[1])
                if name == "sp":
                    for s, v in final_waits:
                        e.wait_ge(s, v)

            @block.tensor
            def _(e):
                run(e, "pe")

            @block.scalar
            def _(e):
                run(e, "act")

            @block.vector
            def _(e):
                run(e, "dve")

            @block.gpsimd
            def _(e):
                run(e, "pool")

            @block.sync
            def _(e):
                run(e, "sp")


def image_catalogue():
    imgs = []
    for j in range(NKV):
        imgs += [("K%d" % j, 1024)]
    imgs += [("V", 2048)]
    for h in range(2):
        imgs += [("GB%d" % h, 1536)]
    for g in range(NG):
        imgs += [("RX%d" % g, 1024)]
    for j in range(NH):
        imgs += [("Q%d" % j, 1024)]
    for j in range(NH):
        imgs += [("AG%d" % j, 1024)]
    for h in range(2):
        imgs += [("GF%d" % h, 1536)]
    for g in range(NG):
        imgs += [("RG%d" % g, 1024)]
    for m in range(8):
        imgs += [("MA%d" % m, 1024), ("MR%d" % m, 1024), ("AP%d" % m, 1024), ("RP%d" % m, 1536)]
    for ch in range(2):
        for kh in range(2):
            imgs += [("WO%d%d" % (ch, kh), 2048)]
    cat, off = {}, 0
    for n, e in imgs:
        cat[n] = (off, e)
        off += e
    return cat, off


CAT, TOT = image_catalogue()


def order_pass1():
    o = []
    for j in range(NKV):
        o += ["K%d" % j, "Kr%d" % j]
    o += ["V"]
    o += ["RX%d" % g for g in range(12)]
    return o


def order_pass2():
    o = []
    for j in range(NH):
        o += ["Q%d" % j, "Qr%d" % j]
    o += ["AG%d" % j for j in range(NH)]
    o += ["RG%d" % g for g in range(12)]
    for m in range(8):
        o += ["MA%d" % m, "MR%d" % m, "AP%d" % m, "RP%d" % m]
    for ch in range(2):
        for kh in range(2):
            o += ["WO%d%d" % (ch, kh)]
    return o


def host_images(inp, depth):
    out = np.zeros((depth, 128, TOT), np.float32)

    def stile(W, c0, kcn=KC, perm=False):
        blk = W[:, c0:c0 + 128]
        if perm:
            blk = np.concatenate([blk[:, 64:128], blk[:, 0:64]], axis=1)
        return blk.reshape(kcn, 128, 128).transpose(1, 0, 2).reshape(128, kcn * 128)

    def put(l, name, arr):
        off, e = CAT[name]
        assert arr.shape == (128, e), (name, arr.shape, e)
        out[l, :, off:off + e] = arr

    for l in range(depth):
        win = inp["w_in"][l]
        for j in range(NKV):
            put(l, "K%d" % j, stile(win, O_K + j * 128))
        put(l, "V", win[:, O_V:O_V + 256].reshape(KC, 128, 256).transpose(1, 0, 2).reshape(128, 2048))
        for d, nm in ((1, "GB"), (0, "GF")):
            for h in range(2):
                a = np.stack([inp["rg_w_a"][l, d, h * 6:(h + 1) * 6], inp["rg_w_x"][l, d, h * 6:(h + 1) * 6]], axis=1)
                put(l, "%s%d" % (nm, h), a.transpose(2, 0, 1, 3).reshape(128, 1536))
        for g in range(NG):
            put(l, "RX%d" % g, stile(win, O_RX + g * 128))
            put(l, "RG%d" % g, stile(win, O_RG + g * 128))
        for j in range(NH):
            put(l, "Q%d" % j, stile(win, O_Q + j * 128))
            put(l, "AG%d" % j, stile(win, O_AG + j * 128))
        for m in range(8):
            put(l, "MA%d" % m, stile(inp["w_merge"][l], m * 128))
            put(l, "MR%d" % m, stile(inp["w_merge"][l], 1024 + m * 128))
            put(l, "AP%d" % m, stile(inp["w_attn_proj"][l], m * 128))
            put(l, "RP%d" % m, stile(inp["w_rnn_proj"][l], m * 128, kcn=12))
        wo = inp["w_out"][l]
        for ch in range(2):
            for kh in range(2):
                a = wo[kh * 512:(kh + 1) * 512, ch * 512:(ch + 1) * 512].reshape(4, 128, 512).transpose(1, 0, 2)
                put(l, "WO%d%d" % (ch, kh), a.reshape(128, 2048))
    return out


def smalls_layout(depth):
    lay, off = {}, 0
    for n, w in (("ng", 8), ("cw", 48), ("cb", 12), ("ba", 24), ("bx", 24), ("lam", 24), ("bm", 16), ("sink", 8)):
        lay[n] = (off, w)
        off += w * depth
    return lay, off


def host_smalls(inp, depth):
    lay, ns = smalls_layout(depth)
    out = np.zeros((128, ns), np.float32)

    def col(v):
        v = np.asarray(v, np.float32)
        n = v.shape[-1] // 128
        return np.moveaxis(v.reshape(v.shape[:-1] + (n, 128)), -1, 0)

    for l in range(depth):
        def put(n, a):
            off, w = lay[n]
            out[:, off + l * w: off + (l + 1) * w] = a.reshape(128, w)
        put("ng", col(inp["norm_gain"][l]))
        put("cw", col(inp["conv_w"][l]))
        put("cb", col(inp["conv_b"][l]))
        put("ba", col(inp["rg_b_a"][l]))
        put("bx", col(inp["rg_b_x"][l]))
        put("lam", col(inp["rg_lambda"][l]))
        put("bm", col(inp["b_merge"][l]))
        put("sink", np.broadcast_to(inp["attn_sink"][l][None, :], (128, 8)))
    return out


def build(seq_lens, depth, debug=False):
    nc = bass.Bass("TRN2", target_bir_lowering=False)
    S = Sched(nc)
    nseq = len(seq_lens)
    SMAX = max(seq_lens)
    lay, NS = smalls_layout(depth)

    x_in = [nc.dram_tensor("x%d" % s, [seq_lens[s], D], F32, kind="ExternalInput").ap() for s in range(nseq)]
    c_in = [nc.dram_tensor("c%d" % s, [128, KC], F32, kind="ExternalInput").ap() for s in range(nseq)]
    wf = nc.dram_tensor("wf", [depth, 128, TOT], F32, kind="ExternalInput").ap()
    wada = nc.dram_tensor("w_ada", [depth, D, 3 * D], F32, kind="ExternalInput").ap()
    bada = nc.dram_tensor("b_ada", [1, depth * 3 * D], F32, kind="ExternalInput").ap()
    smalls_d = nc.dram_tensor("smalls", [128, NS], F32, kind="ExternalInput").ap()
    fgain_d = nc.dram_tensor("fgain", [128, D], F32, kind="ExternalInput").ap()
    y_out = [nc.dram_tensor("y%d" % s, [seq_lens[s], D], F32, kind="ExternalOutput").ap() for s in range(nseq)]

    IK = "ExternalOutput" if debug else "Internal"
    wbf = nc.dram_tensor("wbf", [depth, 128, TOT], BF, kind="Internal").ap()
    cs_d = nc.dram_tensor("cs_d", [128, 2, SMAX], F32, kind=IK).ap()
    xs = [[nc.dram_tensor("xs%d_%d" % (s, b), [seq_lens[s], D], F32, kind="Internal").ap() for b in range(2)]
          for s in range(nseq)]
    kt_d = [nc.dram_tensor("kt%d" % s, [128, NKV, seq_lens[s]], BF, kind=IK).ap() for s in range(nseq)]
    v_d = [nc.dram_tensor("v%d" % s, [seq_lens[s], 256], BF, kind=IK).ap() for s in range(nseq)]
    xc_d = [nc.dram_tensor("xc%d" % s, [128, NG, seq_lens[s]], F32, kind=IK).ap() for s in range(nseq)]
    xcb_d = [nc.dram_tensor("xcbd%d" % s, [128, NG, seq_lens[s]], BF, kind="Internal").ap() for s in range(nseq)]
    hb_d = [nc.dram_tensor("hb%d" % s, [128, NG, seq_lens[s]], F32, kind=IK).ap() for s in range(nseq)]

    SIM = bool(os.environ.get("K_SIM"))
    CFG = dict(NW=8, NT=13, NL=6, NXB=2, NPT=2, NKO=2, NXR=4, NXA=3, LA=5, NMM=8, NTP=0, NBQ=1, NBA=1, NBR=1, NBM=1)
    for kv_ in os.environ.get("K_CFG", "").split(","):
        if "=" in kv_:
            CFG[kv_.split("=")[0]] = int(kv_.split("=")[1])
    sb_state = {"first": None}

    class _Dummy:
        def __getitem__(self, k):
            return self

        def __getattr__(self, k):
            return lambda *a, **kw: self

    def sb(name, shape, dt):
        if SIM:
            return _Dummy()
        return nc.alloc_sbuf_tensor(name, shape, dt)

    xa = [sb("xa%d" % i, [128, D], F32) for i in range(CFG["NXA"])]
    hTs = [[sb("hT%d_%d" % (q_, i), [128, KC, T], BF) for i in range(2)] for q_ in range(nseq)]
    NW = CFG["NW"]
    wring = [sb("wr%d" % i, [128, 1536], BF) for i in range(NW)]
    ident = sb("ident", [128, 128], F32)
    onesf = sb("onesf", [128, 128], F32)
    onesb = sb("onesb", [128, 128], BF)
    smalls = sb("smalls_sb", [128, NS], F32)
    cneg = sb("cneg", [128, depth * 24], F32)
    halfb = sb("halfb", [128, depth * 64], F32)
    NT = CFG["NT"]
    tmp = [sb("tmp%d" % i, [128, T], F32) for i in range(NT)]
    NL = CFG["NL"]
    ldb = [sb("ld%d" % i, [128, T], F32) for i in range(NL)]
    kout = [sb("kout%d" % i, [128, T], BF) for i in range(2)]
    vout = [sb("vout%d" % i, [128, 256], BF) for i in range(2)]
    xb = [sb("xb%d" % i, [128, T + 3], F32) for i in range(CFG["NXB"])]
    xcb = [sb("xcb%d" % i, [128, T], BF) for i in range(CFG["NXB"])]
    css = [sb("cs%d" % q_, [128, 2, T], F32) for q_ in range(nseq)]
    ktw = sb("ktw", [128, NKV, 768], BF)
    vw = sb("vw", [128, 6, 256], BF)
    qTs = [sb("qT%d" % i, [128, NH, T], BF) for i in range(CFG["NBQ"])]
    ags = [sb("ag%d" % i, [128, NH, T], BF) for i in range(CFG["NBQ"])]
    PT = [sb("PT%d" % i, [128, 3, T], BF) for i in range(CFG["NPT"])]
    attnTs = [sb("attnT%d" % i, [128, NH, T], BF) for i in range(CFG["NBA"])]
    rnnTs = [sb("rnnT%d" % i, [128, NG, T], BF) for i in range(CFG["NBR"])]
    mixTs = [sb("mixT%d" % i, [128, KC, T], BF) for i in range(CFG["NBM"])]
    gate_bcs = [sb("gate_bc%d" % q_, [128, D], F32) for q_ in range(nseq)]
    esinks = [sb("esink%d" % q_, [128, NH], F32) for q_ in range(nseq)]
    maskL = sb("maskL", [128, 4, 128], BF)
    maskR = sb("maskR", [128, 4, 128], BF)
    xrbig = sb("xrbig", [128, 4, T], F32)
    xr = [xrbig[:, i, :] for i in range(4)]
    stat = sb("stat", [128, 24], F32)
    carry_xs = [sb("carry_x%d" % q_, [128, NG], F32) for q_ in range(nseq)]
    carry_hs = [sb("carry_h%d" % q_, [128, NG], F32) for q_ in range(nseq)]
    sc_c = sb("sc_c", [128, KC], F32)
    sg_c = sb("sg_c", [128, KC], F32)
    modcs = [sb("modc%d" % q_, [128, 16], F32) for q_ in range(nseq)]
    sgn = sb("sgn", [128, 1], F32)
    ln2c = sb("ln2c", [128, 1], F32)
    invf = sb("invf", [128, 1], F32)
    one11 = sb("one11", [1, 1], F32)
    wadab = xrbig[:, 0:2, :].rearrange("p a (b c) -> p (a b) c", c=128)
    modrow = xrbig[0:1, 2:4, :].rearrange("p a b -> p (a b)")
    wstg = [sb("wstg%d" % i, [128, 1024], BF) for i in range(2)]
    junk = wstg[0]

    tp = [nc.alloc_psum_tensor("tp%d" % i, [128, 4, 128], F32) for i in range(CFG["NTP"])]
    NMM = CFG["NMM"]
    mmb = [(None if (SIM and i >= 8 - CFG["NTP"]) else nc.alloc_psum_tensor("mm%d" % i, [128, T], F32)) for i in range(CFG["NMM"])]
    st = dict(mm=0, tmp=0, ld=0, w=0)

    pending = []

    def defer(fn, reads, writes, key):
        pending.append((fn, reads, writes, key))

    def guard(key):
        for _, r, _, _ in pending:
            if key in r:
                flush()
                return

    def flush():
        for fn, r, w, k in pending:
            S.dma(fn, reads=r, writes=w, key=k)
        pending.clear()

    def next_mm():
        i = st["mm"] % NMM
        st["mm"] += 1
        return mmb[i], ("mm", i)

    def next_tmp():
        i = st["tmp"] % NT
        st["tmp"] += 1
        return tmp[i], ("tmp", i)

    def next_ld():
        i = st["ld"] % NL
        st["ld"] += 1
        guard(("ld", i))
        return ldb[i], ("ld", i)

    ws = dict(used=0)

    def w_get(l, name, sub=0, e_=None):
        i = ws["used"]
        ws["used"] += 1
        off, e = CAT[name]
        off += sub
        if e_ is not None:
            e = e_
        slot = i % NW
        S.dma(lambda en, slot=slot, l=l, off=off, e=e: en.dma_start(out=wring[slot][:, 0:e], in_=wbf[l, :, off:off + e]),
              reads=[("wbf", l, pc_) for pc_ in range(off // 1024, (off + e - 1) // 1024 + 1)], writes=[("w", slot)],
              key=("w", slot), nbytes=256 * e)
        return wring[slot], ("w", slot)

    S.dma(lambda e: e.dma_start(out=smalls[:, :], in_=smalls_d), writes=["smalls"], key="smalls")
    S.op("pool", lambda e: e.memset(onesf[:, :], 1.0), writes=["onesf"])
    S.op("pool", lambda e: e.memset(onesb[:, :], 2.0), writes=["onesb"])
    S.op("pool", lambda e: e.memset(one11[:, :], 1.0), writes=["one11"])
    S.op("pool", lambda e: e.affine_select(out=ident[:, :], in_=onesf[:, :], pattern=[[-1, 128]], compare_op=ALU.is_equal,
                                           fill=0.0, base=0, channel_multiplier=1), reads=["onesf"], writes=["ident"])
    S.op("pool", lambda e: e.memset(maskL[:, :, :], 1.0), writes=["maskL"])
    S.op("pool", lambda e: e.memset(maskR[:, :, :], 1.0), writes=["maskR"])
    S.op("pool", lambda e: e.affine_select(out=maskL[:, :, :], in_=maskL[:, :, :], pattern=[[0, 4], [-1, 128]],
                                           compare_op=ALU.is_ge, fill=0.0, base=0, channel_multiplier=1),
         reads=["maskL"], writes=["maskL"])
    S.op("pool", lambda e: e.affine_select(out=maskR[:, :, :], in_=maskR[:, :, :], pattern=[[0, 4], [1, 128]],
                                           compare_op=ALU.is_ge, fill=0.0, base=0, channel_multiplier=-1),
         reads=["maskR"], writes=["maskR"])
    S.op("pool", lambda e: e.memset(ln2c[:, :], math.log(2.0)), writes=["ln2c"])
    S.op("pool", lambda e: e.memset(sgn[0:64, :], -1.0), writes=["sgn"])
    S.op("pool", lambda e: e.memset(sgn[64:128, :], 1.0), reads=["sgn"], writes=["sgn"])
    lo, lw = lay["lam"]
    S.op("act:exp", lambda e: e.activation(out=cneg[:, :], in_=smalls[:, lo:lo + depth * lw], func=AF.Exp, scale=-1.0),
         reads=["smalls"], writes=["cneg"])
    S.op("act:ln", lambda e: e.activation(out=cneg[:, :], in_=cneg[:, :], func=AF.Ln, bias=1.0, scale=1.0),
         reads=["cneg"], writes=["cneg"])
    S.op("dve", lambda e: e.tensor_scalar(out=cneg[:, :], in0=cneg[:, :], scalar1=-4.0, scalar2=None, op0=ALU.mult),
         reads=["cneg"], writes=["cneg"])
    hb0 = lay["ba"][0]
    hb1 = lay["bx"][0] + depth * lay["bx"][1]
    assert lay["bx"][0] == lay["ba"][0] + depth * lay["ba"][1]
    S.op("dve", lambda e: e.tensor_scalar(out=halfb[:, 0:hb1 - hb0], in0=smalls[:, hb0:hb1], scalar1=0.5, scalar2=None, op0=ALU.mult),
         reads=["smalls"], writes=["halfb"])
    bm0, bmw_ = lay["bm"]
    S.op("dve", lambda e: e.tensor_scalar(out=halfb[:, hb1 - hb0:hb1 - hb0 + depth * bmw_], in0=smalls[:, bm0:bm0 + depth * bmw_],
                                          scalar1=0.5, scalar2=None, op0=ALU.mult),
         reads=["smalls", "halfb"], writes=["halfb"])
    S.op("pool", lambda e: e.iota(invf[0:64, :], pattern=[[0, 1]], base=0, channel_multiplier=1,
                                  allow_small_or_imprecise_dtypes=True), writes=["invf"])
    S.op("pool", lambda e: e.iota(invf[64:128, :], pattern=[[0, 1]], base=0, channel_multiplier=1,
                                  allow_small_or_imprecise_dtypes=True), reads=["invf"], writes=["invf"])
    S.op("act:exp", lambda e: e.activation(out=invf[:, :], in_=invf[:, :], func=AF.Exp, scale=-math.log(10000.0) / 64.0),
         reads=["invf"], writes=["invf"])
    I32 = mybir.dt.int32
    for pc in reversed(range(SMAX // T)):
        pos, kpos = next_tmp()
        S.op("pool", lambda e, pos=pos, pc=pc: e.iota(pos[:, :], pattern=[[1, T]], base=pc * T, channel_multiplier=0,
                                                      allow_small_or_imprecise_dtypes=True), writes=[kpos])
        S.op("dve", lambda e, pos=pos: e.tensor_scalar(out=pos[:, :], in0=pos[:, :], scalar1=invf[:, 0:1], scalar2=None,
                                                       op0=ALU.mult), reads=[kpos, "invf"], writes=[kpos])
        for which in range(2):
            a2, ka2 = next_tmp()
            ki_t, kki = next_tmp()
            kq, kkq = next_tmp()
            ki32 = ki_t[:, :].bitcast(I32)
            shift = math.pi / 2 if which == 0 else 0.0
            S.op("dve", lambda e, a2=a2, pos=pos, shift=shift: e.tensor_scalar(out=a2[:, :], in0=pos[:, :], scalar1=shift,
                                                                               scalar2=None, op0=ALU.add),
                 reads=[kpos], writes=[ka2])
            S.op("dve", lambda e, kq=kq, a2=a2: e.tensor_scalar(out=kq[:, :], in0=a2[:, :], scalar1=1.0 / TWO_PI,
                                                                scalar2=None, op0=ALU.mult), reads=[ka2], writes=[kkq])
            S.op("dve", lambda e, ki32=ki32, kq=kq: e.tensor_copy(out=ki32, in_=kq[:, :]), reads=[kkq], writes=[kki])
            S.op("dve", lambda e, ki32=ki32, kq=kq: e.tensor_copy(out=kq[:, :], in_=ki32), reads=[kki], writes=[kkq])
            for cst in (-C1, -C2):
                S.op("dve", lambda e, a2=a2, kq=kq, cst=cst: e.scalar_tensor_tensor(out=a2[:, :], in0=kq[:, :], scalar=cst,
                                                                                    in1=a2[:, :], op0=ALU.mult, op1=ALU.add),
                     reads=[kkq, ka2], writes=[ka2])
            for thr, cmp_, add in ((math.pi, ALU.is_gt, -TWO_PI), (-math.pi, ALU.is_lt, TWO_PI)):
                S.op("dve", lambda e, a2=a2, kq=kq, thr=thr, cmp_=cmp_: e.tensor_scalar(out=kq[:, :], in0=a2[:, :], scalar1=thr,
                                                                                        scalar2=None, op0=cmp_),
                     reads=[ka2], writes=[kkq])
                S.op("dve", lambda e, a2=a2, kq=kq, add=add: e.scalar_tensor_tensor(out=a2[:, :], in0=kq[:, :], scalar=add,
                                                                                    in1=a2[:, :], op0=ALU.mult, op1=ALU.add),
                     reads=[kkq, ka2], writes=[ka2])
            S.op("dve", lambda e, a2=a2: e.tensor_scalar(out=a2[:, :], in0=a2[:, :], scalar1=math.pi, scalar2=-math.pi,
                                                         op0=ALU.min, op1=ALU.max), reads=[ka2], writes=[ka2])
            S.op("act:sin", lambda e, a2=a2: e.activation(out=a2[:, :], in_=a2[:, :], func=AF.Sin), reads=[ka2], writes=[ka2])
            if which == 1:
                S.op("dve", lambda e, a2=a2: e.tensor_scalar(out=a2[:, :], in0=a2[:, :], scalar1=sgn[:, 0:1], scalar2=None,
                                                             op0=ALU.mult), reads=[ka2, "sgn"], writes=[ka2])
            S.dma(lambda e, a2=a2, which=which, pc=pc: e.dma_start(out=cs_d[:, which, pc * T:(pc + 1) * T], in_=a2[:, :]),
                  reads=[ka2], writes=[("cs_d", pc, which)], key=("cst", ka2[1]))
    PW = 1024
    npieces = (TOT + PW - 1) // PW
    for l in range(depth):
        for pc in range(npieces):
            c0 = pc * PW
            cw = min(PW, TOT - c0)
            q = l * npieces + pc
            sl = q % 3
            par = q % 2
            S.dma(lambda e, sl=sl, l=l, c0=c0, cw=cw: e.dma_start(out=xa[sl][:, 0:cw], in_=wf[l, :, c0:c0 + cw]),
                  writes=[("xa", sl)], key=("xa", sl))
            if par == 0:
                S.op("act", lambda e, par=par, sl=sl, cw=cw: e.activation(out=wstg[par][:, 0:cw], in_=xa[sl][:, 0:cw], func=AF.Copy),
                     reads=[("xa", sl)], writes=[("wstg", par)])
            else:
                S.op("pool", lambda e, par=par, sl=sl, cw=cw: e.tensor_copy(out=wstg[par][:, 0:cw], in_=xa[sl][:, 0:cw]),
                     reads=[("xa", sl)], writes=[("wstg", par)])
            S.dma(lambda e, par=par, l=l, c0=c0, cw=cw: e.dma_start(out=wbf[l, :, c0:c0 + cw], in_=wstg[par][:, 0:cw]),
                  reads=[("wstg", par)], writes=[("wbf", l, pc)], key=("wst", par))

    def stage_A(s, xsrc, i, hs):
        hT, modc = hTs[s], modcs[s]
        for t in range(NB):
            gt = i * NB + t
            sl = st.setdefault("xa_i", 0) % 3
            st["xa_i"] += 1
            guard(("xa", sl))
            col = st.setdefault("st_i", 0) % 8
            st["st_i"] += 1
            S.dma(lambda e, sl=sl, gt=gt: e.dma_start(out=xa[sl][:, :], in_=xsrc[0][gt * 128:(gt + 1) * 128, :]),
                  reads=[(xsrc[1], gt, c2) for c2 in range(2)], writes=[("xa", sl)], key=("xa", sl))
            S.op("act", lambda e, sl=sl, col=col: e.activation(out=junk[:, :], in_=xa[sl][:, :], func=AF.Square,
                                                               accum_out=stat[:, col:col + 1]),
                 reads=[("xa", sl)], writes=[("wstg", 0), ("ss", col)], cost=1250.0)
            S.op("act:sqrt", lambda e, col=col: e.activation(out=stat[:, 8 + col:9 + col], in_=stat[:, col:col + 1], func=AF.Sqrt,
                                                        scale=1.0 / D, bias=EPS), reads=[("ss", col)], writes=[("sd", col)], cost=250.0)
            S.op("dve", lambda e, col=col: e.reciprocal(out=stat[:, 16 + col:17 + col], in_=stat[:, 8 + col:9 + col]),
                 reads=[("sd", col)], writes=[("rs", col)], cost=120.0)
            S.op("pool", lambda e, sl=sl, col=col: e.tensor_scalar(out=xa[sl][:, :], in0=xa[sl][:, :],
                                                                   scalar1=stat[:, 16 + col:17 + col], scalar2=1.0,
                                                                   op0=ALU.mult, op1=ALU.mult),
                 reads=[("xa", sl), ("rs", col)], writes=[("xa", sl)], cost=2500.0)
            for half in range(2):
                if CFG["NTP"] > 0:
                    tpt, tpk = tp[half], ("tp", half)
                else:
                    tpt, tpk = next_mm()

                def tr(e, sl=sl, half=half, tpt=tpt):
                    ins = None
                    for kk in range(4):
                        kc = half * 4 + kk
                        ins = e.transpose(out=tpt[:, kk * 128:(kk + 1) * 128] if CFG["NTP"] == 0 else tpt[:, kk, :],
                                          in_=xa[sl][:, kc * 128:(kc + 1) * 128], identity=ident[:, :])
                    return ins
                S.op("pe", tr, reads=[("xa", sl), "ident"], writes=[tpk], cost=560.0)
                for kk in range(4):
                    kc = half * 4 + kk
                    S.op("act", lambda e, half=half, kk=kk, kc=kc, t=t, tpt=tpt: e.activation(
                        out=hT[hs][:, kc, t * 128:(t + 1) * 128],
                        in_=tpt[:, kk * 128:(kk + 1) * 128] if CFG["NTP"] == 0 else tpt[:, kk, :], func=AF.Identity,
                        scale=modc[:, 8 + kc:9 + kc], bias=modc[:, kc:kc + 1]),
                         reads=[tpk, ("modc", s)], writes=[("hT", s, hs, t, kc)], cost=370.0)

    def hT_keys(s, hs, ts=range(NB)):
        return [("hT", s, hs, t, kc) for t in ts for kc in range(KC)]

    def proj_mm(s, wt, wk, hs, ncols=T, c0=0, woff=0):
        ps, pk = next_mm()
        hT = hTs[s]

        def f(e):
            ins = None
            for kc in range(KC):
                ins = e.matmul(out=ps[:, 0:ncols], lhsT=wt[:, woff + kc * 128: woff + (kc + 1) * 128],
                               rhs=hT[hs][:, kc, c0:c0 + ncols], start=(kc == 0), stop=(kc == KC - 1))
            return ins
        ts = sorted(set([c0 // 128, (c0 + ncols - 1) // 128])) if ncols < T else range(NB)
        S.op("pe", f, reads=[wk] + hT_keys(s, hs, ts), writes=[pk], cost=(2150.0 if ncols >= T else 600.0))
        return ps, pk

    def setup_mod(s, l):
        modc, gate_bc, esink = modcs[s], gate_bcs[s], esinks[s]
        for k_ in range(4):
            guard(("xr", k_))
        S.dma(lambda e: e.dma_start(out=sc_c[:, :], in_=c_in[s]), writes=["sc_c"], key="cin")
        S.op("act:sig", lambda e: e.activation(out=sg_c[:, :], in_=sc_c[:, :], func=AF.Sigmoid), reads=["sc_c"], writes=["sg_c"])
        S.op("dve", lambda e: e.tensor_tensor(out=sc_c[:, :], in0=sc_c[:, :], in1=sg_c[:, :], op=ALU.mult),
             reads=["sc_c", "sg_c"], writes=["sc_c"])
        wv = wada[l].rearrange("(kc p) c -> p kc c", p=128)
        no, nw = lay["ng"]
        for part in range(3):
            S.dma(lambda e, part=part: e.dma_start(out=modrow[:, :], in_=bada[:, l * 3 * D + part * D: l * 3 * D + (part + 1) * D]),
                  writes=[("xr", 2), ("xr", 3)], key="bada")
            for cq in range(8):
                cg = part * 8 + cq
                S.dma(lambda e, cg=cg: e.dma_start(out=wadab[:, :, :], in_=wv[:, :, cg * 128:(cg + 1) * 128]),
                      writes=[("xr", 0), ("xr", 1)], key="wada")
                ps, pk = next_mm()

                def f(e, ps=ps):
                    ins = None
                    for kc in range(KC):
                        ins = e.matmul(out=ps[0:1, 0:128], lhsT=sc_c[:, kc:kc + 1], rhs=wadab[:, kc, :], start=(kc == 0),
                                       stop=(kc == KC - 1))
                    return ins
                S.op("pe", f, reads=["sc_c", ("xr", 0), ("xr", 1)], writes=[pk])
                S.op("dve", lambda e, ps=ps, cq=cq: e.tensor_tensor(out=modrow[0:1, cq * 128:(cq + 1) * 128], in0=ps[0:1, 0:128],
                                                                    in1=modrow[0:1, cq * 128:(cq + 1) * 128], op=ALU.add),
                     reads=[pk, ("xr", 2), ("xr", 3)], writes=[("xr", 2), ("xr", 3)])
            if part < 2:
                ps, pk = next_mm()

                def f2(e, ps=ps):
                    ins = None
                    for j in range(8):
                        ins = e.matmul(out=ps[:, j:j + 1], lhsT=modrow[0:1, j * 128:(j + 1) * 128], rhs=one11[0:1, 0:1],
                                       start=True, stop=True)
                    return ins
                S.op("pe", f2, reads=[("xr", 2), ("xr", 3), "one11"], writes=[pk])
                if part == 0:
                    S.op("dve", lambda e, ps=ps: e.tensor_copy(out=modc[:, 0:8], in_=ps[:, 0:8]), reads=[pk, ("modc", s)], writes=[("modc", s)])
                else:
                    S.op("dve", lambda e, ps=ps: e.scalar_tensor_tensor(out=modc[:, 8:16], in0=ps[:, 0:8], scalar=1.0,
                                                                        in1=smalls[:, no + l * nw: no + (l + 1) * nw],
                                                                        op0=ALU.add, op1=ALU.mult),
                         reads=[pk, "smalls", ("modc", s)], writes=[("modc", s)])
            else:
                for half in range(2):
                    ps, pk = next_mm()
                    S.op("pe", lambda e, ps=ps, half=half: e.matmul(out=ps[:, :], lhsT=onesf[0:1, :],
                                                                    rhs=modrow[0:1, half * T:(half + 1) * T],
                                                                    start=True, stop=True), reads=["onesf", ("xr", 2), ("xr", 3)], writes=[pk])
                    S.op("dve", lambda e, ps=ps, half=half: e.tensor_scalar(out=gate_bc[:, half * T:(half + 1) * T], in0=ps[:, :], scalar1=0.5,
                                                                            scalar2=None, op0=ALU.mult),
                         reads=[pk, ("gate_bc", s)], writes=[("gate_bc", s)])
        so, sw = lay["sink"]
        S.op("act:exp", lambda e: e.activation(out=esink[:, :], in_=smalls[:, so + l * sw: so + (l + 1) * sw], func=AF.Exp,
                                               bias=ln2c[:, 0:1]),
             reads=["smalls", "ln2c"], writes=[("esink", s)], cost=250.0)

    def gates_and_scan(s, l, d, g, xc_t, kxc, xcb_t, kxcb, reverse, first):
        carry_h = carry_hs[s]
        gimg, gk = w_get(l, "%s%d" % ("GB" if d == 1 else "GF", g // 6), sub=(g % 6) * 256, e_=256)
        gg = 0
        psA, pkA = next_mm()
        S.op("pe", lambda e: e.matmul(out=psA[:, :], lhsT=gimg[:, (gg * 2) * 128:(gg * 2 + 1) * 128], rhs=xcb_t[:, :],
                                      start=True, stop=True), reads=[gk, kxcb], writes=[pkA], cost=300.0)
        psX, pkX = next_mm()
        S.op("pe", lambda e: e.matmul(out=psX[:, :], lhsT=gimg[:, (gg * 2 + 1) * 128:(gg * 2 + 2) * 128], rhs=xcb_t[:, :],
                                      start=True, stop=True), reads=[gk, kxcb], writes=[pkX], cost=300.0)
        bo, bw = lay["ba"]
        xo, xw = lay["bx"]
        ci = l * 24 + d * 12 + g
        ra, kra = next_tmp()
        ib, kib = next_tmp()
        sq, ksq = next_tmp()
        hci = ci
        hxi = depth * 24 + ci
        S.op("act:exp", lambda e: e.activation(out=ra[:, :], in_=psA[:, :], func=AF.Tanh, scale=0.5,
                                               bias=halfb[:, hci: hci + 1]), reads=[pkA, "halfb"], writes=[kra])
        S.op("act:exp", lambda e: e.activation(out=ib[:, :], in_=psX[:, :], func=AF.Tanh, scale=0.5,
                                               bias=halfb[:, hxi: hxi + 1]), reads=[pkX, "halfb"], writes=[kib])
        S.op("act:exp", lambda e: e.activation(out=ra[:, :], in_=ra[:, :], func=AF.Exp, scale=cneg[:, ci:ci + 1],
                                               bias=cneg[:, ci:ci + 1]), reads=[kra, "cneg"], writes=[kra])
        S.op("act", lambda e: e.activation(out=sq[:, :], in_=ra[:, :], func=AF.Square), reads=[kra], writes=[ksq])
        S.op("act:sqrt", lambda e: e.activation(out=sq[:, :], in_=sq[:, :], func=AF.Sqrt, scale=-0.25, bias=0.25),
             reads=[ksq], writes=[ksq])
        S.op("dve", lambda e: e.scalar_tensor_tensor(out=ib[:, :], in0=ib[:, :], scalar=1.0, in1=xc_t[:, :],
                                                     op0=ALU.add, op1=ALU.mult),
             reads=[kib, kxc], writes=[kib], cost=1100.0)
        S.op("pool", lambda e: e.tensor_tensor(out=ib[:, :], in0=ib[:, :], in1=sq[:, :], op=ALU.mult),
             reads=[kib, ksq], writes=[kib])
        h, kh = next_ld()
        if reverse:
            S.op("dve", lambda e: e.tensor_tensor_scan(out=h[:, ::-1], data0=ra[:, ::-1], data1=ib[:, ::-1],
                                                       initial=(0.0 if first else carry_h[:, g:g + 1]),
                                                       op0=ALU.mult, op1=ALU.add),
                 reads=[kra, kib, ("ch", s, g)], writes=[kh], cost=1250.0)
            S.op("act", lambda e: e.activation(out=carry_h[:, g:g + 1], in_=h[:, 0:1], func=AF.Copy), reads=[kh],
                 writes=[("ch", s, g)], cost=250.0)
        else:
            S.op("dve", lambda e: e.tensor_tensor_scan(out=h[:, :], data0=ra[:, :], data1=ib[:, :],
                                                       initial=(0.0 if first else carry_h[:, g:g + 1]),
                                                       op0=ALU.mult, op1=ALU.add),
                 reads=[kra, kib, ("ch", s, g)], writes=[kh], cost=1250.0)
            S.op("act", lambda e: e.activation(out=carry_h[:, g:g + 1], in_=h[:, T - 1:T], func=AF.Copy), reads=[kh],
                 writes=[("ch", s, g)], cost=250.0)
        return h, kh

    def pass1(s, l, xsrc):
        hT, cs, carry_x = hTs[s], css[s], carry_xs[s]
        L = seq_lens[s]
        nch = L // T
        stage_A(s, xsrc, nch - 1, (nch - 1) % 2)
        co, cwid = lay["cw"]
        cbo, cbw = lay["cb"]
        def chunk(i):
            hs = i % 2
            first = (i == nch - 1)
            if i > 0:
                stage_A(s, xsrc, i - 1, (i - 1) % 2)
            S.dma(lambda e, i=i: e.dma_start(out=cs[:, :, :], in_=cs_d[:, :, i * T:(i + 1) * T]),
                  reads=[("cs_d", i, 0), ("cs_d", i, 1)], writes=[("cs", s)], key=("cs", s))
            flush()
            for j in range(NKV):
                wt, wk = w_get(l, "K%d" % j)
                ps1, pk1 = proj_mm(s, wt, wk, hs)
                t1, kt1 = next_tmp()
                t2, kt2 = next_tmp()
                S.op("dve", lambda e, t1=t1, ps1=ps1: e.tensor_tensor(out=t1[:, :], in0=ps1[:, :], in1=cs[:, 0, :], op=ALU.mult),
                     reads=[pk1, ("cs", s)], writes=[kt1])
                S.op("dve", lambda e, t2=t2, ps1=ps1: e.tensor_tensor(out=t2[0:64, :], in0=ps1[64:128, :], in1=cs[0:64, 1, :], op=ALU.mult),
                     reads=[pk1, ("cs", s)], writes=[kt2])
                S.op("dve", lambda e, t2=t2, ps1=ps1: e.tensor_tensor(out=t2[64:128, :], in0=ps1[0:64, :], in1=cs[64:128, 1, :], op=ALU.mult),
                     reads=[pk1, ("cs", s), kt2], writes=[kt2])
                ko = st.setdefault("ko", 0) % 2
                st["ko"] += 1
                guard(("kout", ko))
                S.op("pool", lambda e, t1=t1, t2=t2, ko=ko: e.tensor_tensor(out=kout[ko][:, :], in0=t1[:, :], in1=t2[:, :], op=ALU.add),
                     reads=[kt1, kt2], writes=[("kout", ko)])
                defer(lambda e, ko=ko, j=j, i=i: e.dma_start(out=kt_d[s][:, j, i * T:(i + 1) * T], in_=kout[ko][:, :]),
                      [("kout", ko)], [("kt_d", s, i, j)], ("kst", ko))
                yield F1
            wt, wk = w_get(l, "V", sub=0, e_=1024)
            wtb, wkb = w_get(l, "V", sub=1024, e_=1024)
            for t in range(NB):
                ps, pk = next_mm()

                def fv(e, ps=ps, t=t, wt=wt, wtb=wtb):
                    ins = None
                    for kc in range(KC):
                        w_ = wt if kc < 4 else wtb
                        ins = e.matmul(out=ps[:, 0:256], lhsT=hT[hs][:, kc, t * 128:(t + 1) * 128],
                                       rhs=w_[:, (kc % 4) * 256:(kc % 4 + 1) * 256], start=(kc == 0), stop=(kc == KC - 1))
                    return ins
                S.op("pe", fv, reads=[wk, wkb] + hT_keys(s, hs, [t]), writes=[pk], cost=1400.0)
                vo = st.setdefault("vo", 0) % 2
                st["vo"] += 1
                guard(("vout", vo))
                S.op("act", lambda e, ps=ps, vo=vo: e.activation(out=vout[vo][:, :], in_=ps[:, 0:256], func=AF.Copy),
                     reads=[pk], writes=[("vout", vo)])
                defer(lambda e, vo=vo, t=t, i=i: e.dma_start(out=v_d[s][(i * NB + t) * 128:(i * NB + t + 1) * 128, :], in_=vout[vo][:, :]),
                      [("vout", vo)], [("v_d", s, i, t)], ("vst", vo))
                if t % 2 == 1:
                    flush()
            yield F1
            for g in range(NG):
                wt, wk = w_get(l, "RX%d" % g)
                ps, pk = proj_mm(s, wt, wk, hs)
                xc_t, kxc = next_ld()
                cwb = co + l * cwid

                def wcol(tap, g=g, cwb=cwb):
                    return smalls[:, cwb + tap * 12 + g: cwb + tap * 12 + g + 1]
                S.op("dve", lambda e, xc_t=xc_t, ps=ps, g=g, wcol=wcol: e.tensor_scalar(
                    out=xc_t[:, :], in0=ps[:, :], scalar1=wcol(2), scalar2=smalls[:, cbo + l * cbw + g: cbo + l * cbw + g + 1],
                    op0=ALU.mult, op1=ALU.add), reads=[pk, "smalls"], writes=[kxc], cost=720.0)
                for tap, (o0, o1, i0, i1) in ((0, (2, T, 0, T - 2)), (1, (1, T, 0, T - 1)), (3, (0, T - 1, 1, T))):
                    S.op("dve", lambda e, xc_t=xc_t, ps=ps, tap=tap, o0=o0, o1=o1, i0=i0, i1=i1, wcol=wcol: e.scalar_tensor_tensor(
                        out=xc_t[:, o0:o1], in0=ps[:, i0:i1], scalar=wcol(tap), in1=xc_t[:, o0:o1], op0=ALU.mult, op1=ALU.add),
                         reads=[pk, kxc, "smalls"], writes=[kxc], cost=740.0)
                if i > 0:
                    psh, pkh = proj_mm(s, wt, wk, (i - 1) % 2, ncols=2, c0=T - 2)
                    S.op("dve", lambda e, xc_t=xc_t, psh=psh, wcol=wcol: e.scalar_tensor_tensor(
                        out=xc_t[:, 0:2], in0=psh[:, 0:2], scalar=wcol(0), in1=xc_t[:, 0:2], op0=ALU.mult, op1=ALU.add),
                         reads=[pkh, kxc, "smalls"], writes=[kxc], cost=160.0)
                    S.op("dve", lambda e, xc_t=xc_t, psh=psh, wcol=wcol: e.scalar_tensor_tensor(
                        out=xc_t[:, 0:1], in0=psh[:, 1:2], scalar=wcol(1), in1=xc_t[:, 0:1], op0=ALU.mult, op1=ALU.add),
                         reads=[pkh, kxc, "smalls"], writes=[kxc], cost=160.0)
                if not first:
                    S.op("dve", lambda e, xc_t=xc_t, g=g, wcol=wcol: e.scalar_tensor_tensor(
                        out=xc_t[:, T - 1:T], in0=carry_x[:, g:g + 1], scalar=wcol(3), in1=xc_t[:, T - 1:T], op0=ALU.mult, op1=ALU.add),
                         reads=[("cx", s, g), kxc, "smalls"], writes=[kxc], cost=160.0)
                S.op("act", lambda e, ps=ps, g=g: e.activation(out=carry_x[:, g:g + 1], in_=ps[:, 0:1], func=AF.Copy),
                     reads=[pk, kxc], writes=[("cx", s, g)], cost=250.0)
                xi2 = st.setdefault("xcb", 0) % CFG["NXB"]
                st["xcb"] += 1
                guard(("xcb", xi2))
                S.op("pool", lambda e, xc_t=xc_t, xi2=xi2: e.tensor_copy(out=xcb[xi2][:, :], in_=xc_t[:, :]),
                     reads=[kxc], writes=[("xcb", xi2)])
                defer(lambda e, xc_t=xc_t, g=g, i=i: e.dma_start(out=xc_d[s][:, g, i * T:(i + 1) * T], in_=xc_t[:, :]),
                      [kxc], [("xc_d", s, i, g)], ("xcst", kxc[1]))
                defer(lambda e, xi2=xi2, g=g, i=i: e.dma_start(out=xcb_d[s][:, g, i * T:(i + 1) * T], in_=xcb[xi2][:, :]),
                      [("xcb", xi2)], [("xcb_d", s, i, g)], ("xcbst", xi2))
                h, kh = gates_and_scan(s, l, 1, g, xc_t, kxc, xcb[xi2], ("xcb", xi2), True, first)
                defer(lambda e, h=h, g=g, i=i: e.dma_start(out=hb_d[s][:, g, i * T:(i + 1) * T], in_=h[:, :]),
                      [kh], [("hb_d", s, i, g)], ("hbst", kh[1]))
                if g % 2 == 1:
                    flush()
                yield F1
        for i in range(nch - 1, -1, -1):
            yield from chunk(i)
            yield "chunk_end"
        flush()

    def pass2(s, l, xsrc, xdst):
        hT, cs, gate_bc, esink = hTs[s], css[s], gate_bcs[s], esinks[s]
        L = seq_lens[s]
        nch = L // T
        nblk = L // 128
        stage_A(s, xsrc, 0, 0)
        isq = 1.0 / math.sqrt(128.0)
        bmo, bmw = lay["bm"]
        def chunk(i):
            yield "p2_begin"
            hs = i % 2
            first = (i == 0)
            cix = st.setdefault("cix", 0)
            st["cix"] += 1
            bq, ba_, br_, bm_ = cix % CFG["NBQ"], cix % CFG["NBA"], cix % CFG["NBR"], cix % CFG["NBM"]
            qT, ag, attnT, rnnT, mixT = qTs[bq], ags[bq], attnTs[ba_], rnnTs[br_], mixTs[bm_]
            lo_t = max(0, i * T - 128)
            hi_t = min(L, i * T + T + 128)
            woff = lo_t - (i * T - 128)
            wn = hi_t - lo_t
            kreads = [("kt_d", s, ii, j) for ii in (i - 1, i, i + 1) if 0 <= ii < nch for j in range(NKV)]
            vreads = [("v_d", s, ii, t) for ii in (i - 1, i, i + 1) if 0 <= ii < nch for t in range(NB)]
            S.dma(lambda e, lo_t=lo_t, hi_t=hi_t, woff=woff, wn=wn: e.dma_start(out=ktw[:, :, woff:woff + wn], in_=kt_d[s][:, :, lo_t:hi_t]),
                  reads=kreads, writes=["ktw"], key="ktw")
            b0 = lo_t // 128
            nbw = wn // 128
            S.dma(lambda e, lo_t=lo_t, hi_t=hi_t, woff=woff, nbw=nbw: e.dma_start(
                out=vw[:, woff // 128: woff // 128 + nbw, :], in_=v_d[s][lo_t:hi_t, :].rearrange("(b p) c -> p b c", p=128)),
                  reads=vreads, writes=["vw"], key="vw")
            S.dma(lambda e, i=i: e.dma_start(out=cs[:, :, :], in_=cs_d[:, :, i * T:(i + 1) * T]),
                  reads=[("cs_d", i, 0), ("cs_d", i, 1)], writes=[("cs", s)], key=("cs", s))
            if i + 1 < nch:
                stage_A(s, xsrc, i + 1, (i + 1) % 2)
            flush()
            for j in range(NH):
                wt, wk = w_get(l, "Q%d" % j)
                ps1, pk1 = proj_mm(s, wt, wk, hs)
                t1, kt1 = next_tmp()
                t2, kt2 = next_tmp()
                S.op("dve", lambda e, t1=t1, ps1=ps1: e.tensor_tensor(out=t1[:, :], in0=ps1[:, :], in1=cs[:, 0, :], op=ALU.mult),
                     reads=[pk1, ("cs", s)], writes=[kt1])
                S.op("dve", lambda e, t2=t2, ps1=ps1: e.tensor_tensor(out=t2[0:64, :], in0=ps1[64:128, :], in1=cs[0:64, 1, :], op=ALU.mult),
                     reads=[pk1, ("cs", s)], writes=[kt2])
                S.op("dve", lambda e, t2=t2, ps1=ps1: e.tensor_tensor(out=t2[64:128, :], in0=ps1[0:64, :], in1=cs[64:128, 1, :], op=ALU.mult),
                     reads=[pk1, ("cs", s), kt2], writes=[kt2])
                S.op("pool", lambda e, t1=t1, t2=t2, j=j: e.tensor_tensor(out=qT[:, j, :], in0=t1[:, :], in1=t2[:, :], op=ALU.add),
                     reads=[kt1, kt2], writes=[("qT", bq, j)])
                yield F2
            for j in range(NH):
                wt, wk = w_get(l, "AG%d" % j)
                ps, pk = proj_mm(s, wt, wk, hs)
                sg, ksg = next_tmp()
                S.op("act:exp", lambda e, sg=sg, ps=ps: e.activation(out=sg[:, :], in_=ps[:, :], func=AF.Tanh, scale=0.5), reads=[pk], writes=[ksg])
                S.op("dve", lambda e, sg=sg, ps=ps, j=j: e.scalar_tensor_tensor(out=ag[:, j, :], in0=sg[:, :], scalar=1.0, in1=ps[:, :],
                                                                                op0=ALU.add, op1=ALU.mult),
                     reads=[pk, ksg], writes=[("ag", bq, j)], cost=960.0)
                yield F2
            for n in range(NB):
                gb = i * NB + n
                for kv in range(NKV):
                    pi_ = st.setdefault("pt", 0) % CFG["NPT"]
                    st["pt"] += 1
                    ptt = PT[pi_]
                    kbs = [kb for kb in (0, 1, 2) if 0 <= gb + kb - 1 < nblk]
                    for kb in kbs:
                        wc = (n + kb) * 128
                        ps, pk = next_mm()
                        S.op("pe", lambda e, ps=ps, wc=wc, kv=kv, n=n: e.matmul(
                            out=ps[:, :], lhsT=ktw[:, kv, wc:wc + 128], rhs=qT[:, kv * 4:(kv + 1) * 4, n * 128:(n + 1) * 128],
                            start=True, stop=True), reads=["ktw"] + [("qT", bq, kv * 4 + h4) for h4 in range(4)], writes=[pk], cost=300.0)
                        S.op("act:exp", lambda e, ps=ps, ptt=ptt, kb=kb: e.activation(out=ptt[:, kb, :], in_=ps[:, :], func=AF.Exp, scale=isq),
                             reads=[pk], writes=[("PT", pi_, kb)])
                        if kb != 1:
                            mk = maskL if kb == 0 else maskR
                            S.op("pool", lambda e, ptt=ptt, kb=kb, mk=mk: e.tensor_tensor(
                                out=ptt[:, kb, :], in0=ptt[:, kb, :], in1=mk[:, :, :].rearrange("p a b -> p (a b)"), op=ALU.mult),
                                 reads=[("PT", pi_, kb), "maskL", "maskR"], writes=[("PT", pi_, kb)])
                    pso, pko = next_mm()

                    def fo(e, pso=pso, kbs=kbs, n=n, kv=kv, ptt=ptt):
                        ins = None
                        for q_, kb in enumerate(kbs):
                            ins = e.matmul(out=pso[:, :], lhsT=vw[:, n + kb, kv * 128:(kv + 1) * 128], rhs=ptt[:, kb, :],
                                           start=(q_ == 0), stop=(q_ == len(kbs) - 1))
                        return ins
                    S.op("pe", fo, reads=["vw"] + [("PT", pi_, kb) for kb in kbs], writes=[pko], cost=270.0 * len(kbs))
                    psd, pkd = next_mm()

                    def fd(e, psd=psd, kbs=kbs, ptt=ptt):
                        ins = None
                        for q_, kb in enumerate(kbs):
                            ins = e.matmul(out=psd[:, :], lhsT=onesb[:, :], rhs=ptt[:, kb, :],
                                           start=(q_ == 0), stop=(q_ == len(kbs) - 1))
                        return ins
                    S.op("pe", fd, reads=["onesb"] + [("PT", pi_, kb) for kb in kbs], writes=[pkd], cost=270.0 * len(kbs))
                    den, kden = next_tmp()
                    S.op("dve", lambda e, den=den, psd=psd, kv=kv: e.tensor_tensor(
                        out=den[:, :].rearrange("p (a b) -> p a b", a=4), in0=psd[:, :].rearrange("p (a b) -> p a b", a=4),
                        in1=esink[:, kv * 4:(kv + 1) * 4, None].broadcast_to([128, 4, 128]), op=ALU.add),
                         reads=[pkd, ("esink", s)], writes=[kden])
                    S.op("dve", lambda e, den=den: e.reciprocal(out=den[:, :], in_=den[:, :]), reads=[kden], writes=[kden], cost=1660.0)
                    S.op("pool", lambda e, den=den, kv=kv, n=n: e.tensor_tensor(
                        out=den[:, :].rearrange("p (a b) -> p a b", a=4), in0=den[:, :].rearrange("p (a b) -> p a b", a=4),
                        in1=ag[:, kv * 4:(kv + 1) * 4, n * 128:(n + 1) * 128], op=ALU.mult),
                         reads=[kden] + [("ag", bq, kv * 4 + h4) for h4 in range(4)], writes=[kden])
                    S.op("dve", lambda e, den=den, pso=pso, kv=kv, n=n: e.tensor_tensor(
                        out=attnT[:, kv * 4:(kv + 1) * 4, n * 128:(n + 1) * 128], in0=pso[:, :].rearrange("p (a b) -> p a b", a=4),
                        in1=den[:, :].rearrange("p (a b) -> p a b", a=4), op=ALU.mult),
                         reads=[pko, kden], writes=[("attnT", ba_, kv, n)])
                    yield F2
            def rnn_loads(g):
                xc_t, kxc = next_ld()
                hb_t, khb = next_ld()
                S.dma(lambda e, xc_t=xc_t, g=g: e.dma_start(out=xc_t[:, :], in_=xc_d[s][:, g, i * T:(i + 1) * T]),
                      reads=[("xc_d", s, i, g)], writes=[kxc], key=("xcl", kxc[1]))
                S.dma(lambda e, hb_t=hb_t, g=g: e.dma_start(out=hb_t[:, :], in_=hb_d[s][:, g, i * T:(i + 1) * T]),
                      reads=[("hb_d", s, i, g)], writes=[khb], key=("hbl", khb[1]))
                return xc_t, kxc, hb_t, khb
            for g in range(NG):
                xc_t, kxc, hb_t, khb = rnn_loads(g)
                xi2 = st.setdefault("xcb", 0) % CFG["NXB"]
                st["xcb"] += 1
                guard(("xcb", xi2))
                S.dma(lambda e, xi2=xi2, g=g: e.dma_start(out=xcb[xi2][:, :], in_=xcb_d[s][:, g, i * T:(i + 1) * T]),
                      reads=[("xcb_d", s, i, g)], writes=[("xcb", xi2)], key=("xcbl", xi2), nbytes=131072)
                wt, wk = w_get(l, "RG%d" % g)
                ps, pk = proj_mm(s, wt, wk, hs)
                sg, ksg = next_tmp()
                S.op("act:exp", lambda e, sg=sg, ps=ps: e.activation(out=sg[:, :], in_=ps[:, :], func=AF.Tanh, scale=0.5), reads=[pk], writes=[ksg])
                S.op("dve", lambda e, sg=sg, ps=ps: e.scalar_tensor_tensor(out=sg[:, :], in0=sg[:, :], scalar=1.0, in1=ps[:, :],
                                                                           op0=ALU.add, op1=ALU.mult),
                     reads=[pk, ksg], writes=[ksg], cost=960.0)
                h, kh = gates_and_scan(s, l, 0, g, xc_t, kxc, xcb[xi2], ("xcb", xi2), False, first)
                S.op("pool", lambda e, h=h, hb_t=hb_t: e.tensor_tensor(out=h[:, :], in0=h[:, :], in1=hb_t[:, :], op=ALU.add),
                     reads=[kh, khb], writes=[kh])
                S.op("dve", lambda e, h=h, sg=sg, g=g: e.scalar_tensor_tensor(out=rnnT[:, g, :], in0=h[:, :], scalar=0.5, in1=sg[:, :],
                                                                              op0=ALU.mult, op1=ALU.mult),
                     reads=[kh, ksg], writes=[("rnnT", br_, g)], cost=1100.0)
                yield F2
            for m in range(8):
                wa, wka = w_get(l, "MA%d" % m)
                psa, pka = proj_mm(s, wa, wka, hs)
                wr, wkr = w_get(l, "MR%d" % m)
                psr, pkr = proj_mm(s, wr, wkr, hs)
                ga, kga = next_tmp()
                gr, kgr = next_tmp()
                hbm = depth * 48 + l * 16
                S.op("act:exp", lambda e, ga=ga, psa=psa, m=m: e.activation(out=ga[:, :], in_=psa[:, :], func=AF.Tanh, scale=0.5,
                                                                        bias=halfb[:, hbm + m: hbm + m + 1]),
                     reads=[pka, "halfb"], writes=[kga])
                S.op("act:exp", lambda e, gr=gr, psr=psr, m=m: e.activation(out=gr[:, :], in_=psr[:, :], func=AF.Tanh, scale=0.5,
                                                                        bias=halfb[:, hbm + 8 + m: hbm + 9 + m]),
                     reads=[pkr, "halfb"], writes=[kgr])
                wp, wkp = w_get(l, "AP%d" % m)
                psp, pkp = next_mm()

                def fp(e, psp=psp, wp=wp):
                    ins = None
                    for kc in range(8):
                        ins = e.matmul(out=psp[:, :], lhsT=wp[:, kc * 128:(kc + 1) * 128], rhs=attnT[:, kc, :],
                                       start=(kc == 0), stop=(kc == 7))
                    return ins
                S.op("pe", fp, reads=[wkp] + [("attnT", ba_, kv, n) for kv in range(NKV) for n in range(NB)], writes=[pkp])
                wq, wkq = w_get(l, "RP%d" % m)
                psq, pkq = next_mm()

                def fq(e, psq=psq, wq=wq):
                    ins = None
                    for kc in range(NG):
                        ins = e.matmul(out=psq[:, :], lhsT=wq[:, kc * 128:(kc + 1) * 128], rhs=rnnT[:, kc, :],
                                       start=(kc == 0), stop=(kc == NG - 1))
                    return ins
                S.op("pe", fq, reads=[wkq] + [("rnnT", br_, g) for g in range(NG)], writes=[pkq], cost=3200.0)
                S.op("dve", lambda e, ga=ga, psp=psp: e.scalar_tensor_tensor(out=ga[:, :], in0=ga[:, :], scalar=1.0, in1=psp[:, :],
                                                                             op0=ALU.add, op1=ALU.mult),
                     reads=[pkp, kga], writes=[kga], cost=960.0)
                S.op("dve", lambda e, gr=gr, psq=psq: e.scalar_tensor_tensor(out=gr[:, :], in0=gr[:, :], scalar=1.0, in1=psq[:, :],
                                                                             op0=ALU.add, op1=ALU.mult),
                     reads=[pkq, kgr], writes=[kgr], cost=960.0)
                S.op("pool", lambda e, ga=ga, gr=gr, m=m: e.tensor_tensor(out=mixT[:, m, :], in0=ga[:, :], in1=gr[:, :], op=ALU.add),
                     reads=[kga, kgr], writes=[("mixT", bm_, m)])
                yield F2
            for ch in range(2):
                wos = [w_get(l, "WO%d%d" % (ch, q_ // 2), sub=(q_ % 2) * 1024, e_=1024) for q_ in range(4)]
                for t in range(NB):
                    gt = i * NB + t
                    xi = st.setdefault("xr", 0) % 4
                    st["xr"] += 1
                    guard(("xr", xi))
                    S.dma(lambda e, xi=xi, gt=gt, ch=ch: e.dma_start(out=xr[xi][:, :], in_=xsrc[0][gt * 128:(gt + 1) * 128, ch * T:(ch + 1) * T]),
                          reads=[(xsrc[1], gt, c2) for c2 in range(2)], writes=[("xr", xi)], key=("xr", xi))
                    ps, pk = next_mm()

                    def fw(e, ps=ps, t=t, wos=wos):
                        ins = None
                        for kc in range(KC):
                            wt_ = wos[kc // 2][0]
                            ins = e.matmul(out=ps[:, :], lhsT=mixT[:, kc, t * 128:(t + 1) * 128],
                                           rhs=wt_[:, (kc % 2) * 512:(kc % 2 + 1) * 512], start=(kc == 0), stop=(kc == KC - 1))
                        return ins
                    S.op("pe", fw, reads=[w_[1] for w_ in wos] + [("mixT", bm_, m) for m in range(8)], writes=[pk])
                    y, ky = next_tmp()
                    S.op("dve", lambda e, y=y, ps=ps, ch=ch: e.tensor_tensor(out=y[:, :], in0=ps[:, :], in1=gate_bc[:, ch * T:(ch + 1) * T],
                                                                           op=ALU.mult), reads=[pk, ("gate_bc", s)], writes=[ky])
                    S.op("pool", lambda e, y=y, xi=xi: e.tensor_tensor(out=xr[xi][:, :], in0=y[:, :], in1=xr[xi][:, :], op=ALU.add),
                         reads=[ky, ("xr", xi)], writes=[("xr", xi)])
                    defer(lambda e, xi=xi, gt=gt, ch=ch: e.dma_start(out=xdst[0][gt * 128:(gt + 1) * 128, ch * T:(ch + 1) * T], in_=xr[xi][:, :]),
                          [("xr", xi)], [(xdst[1], gt, ch)], ("xst", xi))
                    if t % 2 == 1:
                        flush()
                yield F2
        for i in range(nch):
            yield from chunk(i)
            yield "chunk_end"
        flush()

    def pass3(s, xsrc):
        fg_bc = gate_bcs[s]
        S.dma(lambda e: e.dma_start(out=fg_bc[:, :], in_=fgain_d), writes=[("gate_bc", s)], key=("fg", s))
        L = seq_lens[s]
        for gt in range(L // 128):
            sl = st.setdefault("xa_i", 0) % 3
            st["xa_i"] += 1
            guard(("xa", sl))
            col = st.setdefault("st_i", 0) % 8
            st["st_i"] += 1
            S.dma(lambda e, sl=sl, gt=gt: e.dma_start(out=xa[sl][:, :], in_=xsrc[0][gt * 128:(gt + 1) * 128, :]),
                  reads=[(xsrc[1], gt, c2) for c2 in range(2)], writes=[("xa", sl)], key=("xa", sl))
            if gt >= 1:
                flush()
            S.op("act", lambda e, sl=sl, col=col: e.activation(out=junk[:, :], in_=xa[sl][:, :], func=AF.Square,
                                                               accum_out=stat[:, col:col + 1]),
                 reads=[("xa", sl)], writes=[("wstg", 0), ("ss", col)], cost=1250.0)
            S.op("act:sqrt", lambda e, col=col: e.activation(out=stat[:, 8 + col:9 + col], in_=stat[:, col:col + 1], func=AF.Sqrt,
                                                        scale=1.0 / D, bias=EPS), reads=[("ss", col)], writes=[("sd", col)], cost=250.0)
            S.op("dve", lambda e, col=col: e.reciprocal(out=stat[:, 16 + col:17 + col], in_=stat[:, 8 + col:9 + col]),
                 reads=[("sd", col)], writes=[("rs", col)], cost=120.0)
            S.op("dve", lambda e, sl=sl, col=col: e.scalar_tensor_tensor(out=xa[sl][:, :], in0=xa[sl][:, :],
                                                                         scalar=stat[:, 16 + col:17 + col], in1=fg_bc[:, :],
                                                                         op0=ALU.mult, op1=ALU.mult),
                 reads=[("xa", sl), ("rs", col), ("gate_bc", s)], writes=[("xa", sl)], cost=1250.0)
            defer(lambda e, sl=sl, gt=gt: e.dma_start(out=y_out[s][gt * 128:(gt + 1) * 128, :], in_=xa[sl][:, :]),
                  [("xa", sl)], [("y", s, gt)], ("yst", sl))
            yield 1
        flush()

    def stream(s):
        for l in range(depth):
            xsrc = (x_in[s], ("xin", s)) if l == 0 else (xs[s][(l - 1) % 2], ("xs", s, (l - 1) % 2))
            xdst = (xs[s][l % 2], ("xs", s, l % 2))
            setup_mod(s, l)
            yield from pass1(s, l, xsrc)
            yield from pass2(s, l, xsrc, xdst)
        yield from pass3(s, (xs[s][(depth - 1) % 2], ("xs", s, (depth - 1) % 2)))

    gens = [stream(s) for s in range(nseq)]
    alive = [True] * nseq
    prog_ = [0.0] * nseq
    nchs = [seq_lens[s] // T for s in range(nseq)]

    in_p2 = [False] * nseq
    blocked = [False] * nseq

    def step(s):
        if blocked[s]:
            if any(in_p2[o] for o in range(nseq) if o != s):
                o = [o for o in range(nseq) if o != s and in_p2[o]][0]
                step(o)
                return
            blocked[s] = False
            in_p2[s] = True
        try:
            tok = next(gens[s])
        except StopIteration:
            alive[s] = False
            in_p2[s] = False
            return
        if tok == "p2_begin":
            if any(in_p2[o] for o in range(nseq) if o != s):
                blocked[s] = True
            else:
                in_p2[s] = True
        elif tok == "chunk_end":
            in_p2[s] = False
            prog_[s] = math.floor(prog_[s]) + 1.0
        else:
            prog_[s] = min(prog_[s] + float(tok), math.floor(prog_[s]) + 0.99)

    if nseq == 2 and os.environ.get("K_INTERLEAVE", "1") == "1":
        lead = float(nchs[0])
        while alive[0] and prog_[0] < lead:
            step(0)
        while alive[0] or alive[1]:
            if not alive[1]:
                step(0)
            elif not alive[0]:
                step(1)
            else:
                r0 = (prog_[0] - lead) / nchs[0]
                r1 = prog_[1] / nchs[1]
                step(0 if r0 <= r1 else 1)
    else:
        for s in range(nseq):
            while alive[s]:
                step(s)
    flush()
    if SIM:
        S._schedule()
        S.n_sems = S.max_val = 0
    else:
        S.emit()
    return nc, S


_CACHE = {}


def run(inputs, seq_lens, depth, n_cores, seq_of_core, debug=False):
    key = (tuple(seq_lens), depth, debug)
    if key not in _CACHE:
        _CACHE[key] = build(seq_lens, depth, debug)[0]
    nc = _CACHE[key]
    wfh = host_images(inputs, depth)
    sm = host_smalls(inputs, depth)
    fg = np.ascontiguousarray(np.broadcast_to(np.asarray(inputs["final_gain"], np.float32)[None, :], (128, D)))
    bada = np.ascontiguousarray(np.asarray(inputs["b_ada"], np.float32)[:depth].reshape(1, depth * 3 * D))
    wada = np.ascontiguousarray(np.asarray(inputs["w_ada"], np.float32)[:depth])
    in_maps = []
    for c in range(n_cores):
        m = {"wf": wfh, "w_ada": wada, "b_ada": bada, "smalls": sm, "fgain": fg}
        for s, (xn, cn, idx) in enumerate(seq_of_core(c)):
            m["x%d" % s] = np.ascontiguousarray(np.asarray(inputs[xn][idx], np.float32))
            m["c%d" % s] = np.ascontiguousarray(np.asarray(inputs[cn][idx], np.float32).reshape(KC, 128).T)
        in_maps.append(m)
    res = run_bass_kernel_spmd(nc, in_maps, core_ids=list(range(n_cores)))
    return res.results


def kernel(**inputs):
    inputs = {k: np.asarray(v) for k, v in inputs.items()}
    B, SP = inputs["x_prompt"].shape[0], inputs["x_prompt"].shape[1]
    BS, SS = inputs["x_sample"].shape[0], inputs["x_sample"].shape[1]
    depth = inputs["w_in"].shape[0]
    n = 8
    res = run(inputs, [SP, SS], depth, n,
              lambda c: [("x_prompt", "c_prompt", c % B), ("x_sample", "c_sample", c % BS)])
    yp = np.stack([res[c]["y0"] for c in range(B)], axis=0).astype(np.float32)
    ysm = np.stack([res[c]["y1"] for c in range(BS)], axis=0).astype(np.float32)
    return (yp, ysm)
```

```python
import math
import os
import numpy as np
import concourse.bass as bass
import concourse.mybir as mybir
from concourse.bass_utils import run_bass_kernel_spmd

F32 = mybir.dt.float32
BF = mybir.dt.bfloat16
AF = mybir.ActivationFunctionType
ALU = mybir.AluOpType

D = 1024
KC = 8
DIN = 5632
NH = 8
NKV = 2
DR = 1536
NG = 12
T = 512
NB = 4
EPS = 1e-6
O_Q, O_K, O_V, O_AG, O_RX, O_RG = 0, 1024, 1280, 1536, 2560, 4096
PEN = float(os.environ.get("K_PEN", "1300"))
PEN_DEC = float(os.environ.get("K_PEND", "600"))
F1 = float(os.environ.get("K_F1", "0.015"))
F2 = float(os.environ.get("K_F2", "0.03"))
TWO_PI = 2.0 * math.pi
C1 = 6.28125
C2 = TWO_PI - C1


class Sched:
    ENG = ("pe", "act", "dve", "pool", "sp")
    DEFCOST = {"pe": 2150.0, "act": 660.0, "dve": 720.0, "pool": 1350.0, "sp": 350.0}

    def __init__(self, nc):
        self.nc = nc
        self.ops = []
        self.buf = {}
        self.reorder = True

    def _add(self, eng, prod, fn, reads, writes, inc, cost, lat, tab):
        idx = len(self.ops)
        sem_preds = set()
        ord_preds = set()
        for k in reads:
            st = self.buf.get(k)
            if st is not None and st[0] is not None:
                sem_preds.add(st[0])
        for k in writes:
            st = self.buf.get(k)
            if st is not None:
                if st[0] is not None:
                    sem_preds.add(st[0])
                for r in st[1]:
                    if r[0] == prod and not isinstance(prod, tuple):
                        ord_preds.add(r)
                    else:
                        sem_preds.add(r)
        if prod == "pe":
            pe_only = set(p for p in sem_preds if p[0] == "pe")
            sem_preds -= pe_only
            ord_preds |= pe_only
        if cost is None:
            cost = self.DEFCOST[eng]
        import sys as _sys
        fr = _sys._getframe(2)
        while fr.f_code.co_name in ("op", "dma", "flush", "defer"):
            fr = fr.f_back
        self.ops[-1:] = self.ops[-1:]
        self._tag = fr.f_lineno
        self.ops.append(dict(tag=self._tag, eng=eng, prod=prod, fn=fn, sem=sem_preds, ordp=ord_preds, inc=inc, sig=False,
                             cost=cost, lat=(cost if lat is None else lat), tab=tab))
        for k in reads:
            st = self.buf.setdefault(k, [None, []])
            st[1].append((prod, idx))
        for k in writes:
            self.buf[k] = [(prod, idx), []]
        return idx

    def op(self, eng, fn, reads=(), writes=(), cost=None, tab=None):
        if ":" in eng:
            eng, tab = eng.split(":")
        return self._add(eng, eng, fn, list(reads), list(writes), 1, cost, None, tab)

    def dma(self, fn, reads=(), writes=(), key=None, eng="sp", nbytes=262144):
        lat = 2300.0 + nbytes / 120.0
        return self._add(eng, ("dma", key), fn, list(reads), list(writes), 16, 350.0, lat, None)

    def _schedule(self):
        import heapq
        ops = self.ops
        n = len(ops)
        succs = [[] for _ in range(n)]
        npred = [0] * n
        for i, o in enumerate(ops):
            ps = set(p[1] for p in o["sem"]) | set(p[1] for p in o["ordp"])
            o["allp"] = ps
            npred[i] = len(ps)
            for p in ps:
                succs[p].append(i)
        ready_t = [0.0] * n
        finish = [0.0] * n
        blev = [0.0] * n
        for i in range(n - 1, -1, -1):
            m_ = 0.0
            for s_ in succs[i]:
                if blev[s_] > m_:
                    m_ = blev[s_]
            blev[i] = m_ + ops[i]["lat"]
        PRI = os.environ.get("K_PRI", "blev")
        SLACK = float(os.environ.get("K_SLACK", "0"))
        ALPHA = float(os.environ.get("K_ALPHA", "0.001"))
        heaps = {e: [] for e in self.ENG}
        tfree = {e: 0.0 for e in self.ENG}
        curtab = [None]
        for i in range(n):
            if npred[i] == 0:
                heapq.heappush(heaps[ops[i]["eng"]], (0.0, i))
        order = []
        WIN = int(os.environ.get("K_WIN", "24"))
        done = 0
        while done < n:
            best = None
            for e in self.ENG:
                h = heaps[e]
                if not h:
                    continue
                cand = heapq.nsmallest(WIN, h)
                tf = tfree[e]
                pick = None
                for rt, i in cand:
                    stt = max(rt, tf)
                    pen = 0.0
                    pend = 0.0
                    if e == "act":
                        tb = ops[i]["tab"]
                        if tb is not None and tb != curtab[0]:
                            pen = PEN
                            pend = PEN_DEC - PEN
                    if PRI == "hyb":
                        key = (stt + pen - ALPHA * blev[i], i)
                    elif PRI == "blev":
                        key = (stt + pen + pend, -blev[i], i)
                    else:
                        key = (round((stt + pen) / max(SLACK, 1.0)) if SLACK > 0 else stt + pen, i)
                    if pick is None or key < pick[0]:
                        pick = (key, rt, i, stt + pen)
                if best is None or (pick[3], pick[2]) < (best[0][3], best[0][2]):
                    best = (pick, e)
            (key, rt, i, stt), e = best
            heaps[e].remove((rt, i))
            heapq.heapify(heaps[e])
            o = ops[i]
            if e == "act" and o["tab"] is not None:
                if curtab[0] != o["tab"]:
                    self.nswitch = getattr(self, "nswitch", 0) + 1
                curtab[0] = o["tab"]
            if getattr(self, "diag", None) is not None and stt > tfree[e] + 1.0:
                bp = max(o["allp"], key=lambda p_: finish[p_]) if o["allp"] else None
                kk = (e, o["tag"], ops[bp]["eng"] if bp is not None else None, ops[bp]["tag"] if bp is not None else None)
                self.diag[kk] = self.diag.get(kk, 0.0) + (stt - tfree[e])
            tfree[e] = stt + o["cost"]
            finish[i] = stt + o["lat"]
            order.append(i)
            done += 1
            for s_ in succs[i]:
                npred[s_] -= 1
                if finish[i] > ready_t[s_]:
                    ready_t[s_] = finish[i]
                if npred[s_] == 0:
                    heapq.heappush(heaps[ops[s_]["eng"]], (ready_t[s_], s_))
        self.sim_time = max(finish) if finish else 0.0
        return order

    def emit(self):
        nc = self.nc
        ops = self.ops
        order = self._schedule() if self.reorder else list(range(len(ops)))
        for o in ops:
            for pp, pi in o["sem"]:
                ops[pi]["sig"] = True
        sems, cnt = {}, {}
        val = [0] * len(ops)
        for i in order:
            o = ops[i]
            p = o["prod"]
            if isinstance(p, tuple):
                o["sig"] = True
            if o["sig"]:
                if p not in sems:
                    sems[p] = nc.alloc_semaphore("s%d" % len(sems))
                    cnt[p] = 0
                cnt[p] += o["inc"]
                val[i] = cnt[p]
        prog = {e: [] for e in self.ENG}
        seen = {e: {} for e in self.ENG}
        for i in order:
            o = ops[i]
            e = o["eng"]
            need = {}
            for pp, pi in o["sem"]:
                v = val[pi]
                assert v > 0
                if need.get(pp, 0) < v:
                    need[pp] = v
            waits = []
            for pp, v in need.items():
                if seen[e].get(pp, 0) < v:
                    seen[e][pp] = v
                    waits.append((sems[pp], v))
            prog[e].append((waits, o["fn"], (sems[o["prod"]], o["inc"]) if o["sig"] else None))
        final_waits = [(sems[p], cnt[p]) for p in sems if isinstance(p, tuple)]
        self.n_sems = len(sems)
        self.max_val = max(cnt.values()) if cnt else 0

        with nc.Block() as block:
            def run(e, name):
                for waits, fn, inc in prog[name]:
                    for s, v in waits:
                        e.wait_ge(s, v)
                    ins = fn(e)
                    if inc is not None:
                        ins.then_inc(inc[0], inc[1])
                if name == "sp":
                    for s, v in final_waits:
                        e.wait_ge(s, v)

            @block.tensor
            def _(e):
                run(e, "pe")

            @block.scalar
            def _(e):
                run(e, "act")

            @block.vector
            def _(e):
                run(e, "dve")

            @block.gpsimd
            def _(e):
                run(e, "pool")

            @block.sync
            def _(e):
                run(e, "sp")


def image_catalogue():
    imgs = []
    for j in range(NKV):
        imgs += [("K%d" % j, 1024)]
    imgs += [("V", 2048)]
    for h in range(2):
        imgs += [("GB%d" % h, 1536)]
    for g in range(NG):
        imgs += [("RX%d" % g, 1024)]
    for j in range(NH):
        imgs += [("Q%d" % j, 1024)]
    for j in range(NH):
        imgs += [("AG%d" % j, 1024)]
    for h in range(2):
        imgs += [("GF%d" % h, 1536)]
    for g in range(NG):
        imgs += [("RG%d" % g, 1024)]
    for m in range(8):
        imgs += [("MA%d" % m, 1024), ("MR%d" % m, 1024), ("AP%d" % m, 1024), ("RP%d" % m, 1536)]
    for ch in range(2):
        for kh in range(2):
            imgs += [("WO%d%d" % (ch, kh), 2048)]
    cat, off = {}, 0
    for n, e in imgs:
        cat[n] = (off, e)
        off += e
    return cat, off


CAT, TOT = image_catalogue()


def order_pass1():
    o = []
    for j in range(NKV):
        o += ["K%d" % j, "Kr%d" % j]
    o += ["V"]
    o += ["RX%d" % g for g in range(12)]
    return o


def order_pass2():
    o = []
    for j in range(NH):
        o += ["Q%d" % j, "Qr%d" % j]
    o += ["AG%d" % j for j in range(NH)]
    o += ["RG%d" % g for g in range(12)]
    for m in range(8):
        o += ["MA%d" % m, "MR%d" % m, "AP%d" % m, "RP%d" % m]
    for ch in range(2):
        for kh in range(2):
            o += ["WO%d%d" % (ch, kh)]
    return o


def host_images(inp, depth):
    out = np.zeros((depth, 128, TOT), np.float32)

    def stile(W, c0, kcn=KC, perm=False):
        blk = W[:, c0:c0 + 128]
        if perm:
            blk = np.concatenate([blk[:, 64:128], blk[:, 0:64]], axis=1)
        return blk.reshape(kcn, 128, 128).transpose(1, 0, 2).reshape(128, kcn * 128)

    def put(l, name, arr):
        off, e = CAT[name]
        assert arr.shape == (128, e), (name, arr.shape, e)
        out[l, :, off:off + e] = arr

    for l in range(depth):
        win = inp["w_in"][l]
        for j in range(NKV):
            put(l, "K%d" % j, stile(win, O_K + j * 128))
        put(l, "V", win[:, O_V:O_V + 256].reshape(KC, 128, 256).transpose(1, 0, 2).reshape(128, 2048))
        for d, nm in ((1, "GB"), (0, "GF")):
            for h in range(2):
                a = np.stack([inp["rg_w_a"][l, d, h * 6:(h + 1) * 6], inp["rg_w_x"][l, d, h * 6:(h + 1) * 6]], axis=1)
                put(l, "%s%d" % (nm, h), a.transpose(2, 0, 1, 3).reshape(128, 1536))
        for g in range(NG):
            put(l, "RX%d" % g, stile(win, O_RX + g * 128))
            put(l, "RG%d" % g, stile(win, O_RG + g * 128))
        for j in range(NH):
            put(l, "Q%d" % j, stile(win, O_Q + j * 128))
            put(l, "AG%d" % j, stile(win, O_AG + j * 128))
        for m in range(8):
            put(l, "MA%d" % m, stile(inp["w_merge"][l], m * 128))
            put(l, "MR%d" % m, stile(inp["w_merge"][l], 1024 + m * 128))
            put(l, "AP%d" % m, stile(inp["w_attn_proj"][l], m * 128))
            put(l, "RP%d" % m, stile(inp["w_rnn_proj"][l], m * 128, kcn=12))
        wo = inp["w_out"][l]
        for ch in range(2):
            for kh in range(2):
                a = wo[kh * 512:(kh + 1) * 512, ch * 512:(ch + 1) * 512].reshape(4, 128, 512).transpose(1, 0, 2)
                put(l, "WO%d%d" % (ch, kh), a.reshape(128, 2048))
    return out


def smalls_layout(depth):
    lay, off = {}, 0
    for n, w in (("ng", 8), ("cw", 48), ("cb", 12), ("ba", 24), ("bx", 24), ("lam", 24), ("bm", 16), ("sink", 8)):
        lay[n] = (off, w)
        off += w * depth
    return lay, off


def host_smalls(inp, depth):
    lay, ns = smalls_layout(depth)
    out = np.zeros((128, ns), np.float32)

    def col(v):
        v = np.asarray(v, np.float32)
        n = v.shape[-1] // 128
        return np.moveaxis(v.reshape(v.shape[:-1] + (n, 128)), -1, 0)

    for l in range(depth):
        def put(n, a):
            off, w = lay[n]
            out[:, off + l * w: off + (l + 1) * w] = a.reshape(128, w)
        put("ng", col(inp["norm_gain"][l]))
        put("cw", col(inp["conv_w"][l]))
        put("cb", col(inp["conv_b"][l]))
        put("ba", col(inp["rg_b_a"][l]))
        put("bx", col(inp["rg_b_x"][l]))
        put("lam", col(inp["rg_lambda"][l]))
        put("bm", col(inp["b_merge"][l]))
        put("sink", np.broadcast_to(inp["attn_sink"][l][None, :], (128, 8)))
    return out


def build(seq_lens, depth, debug=False):
    nc = bass.Bass("TRN2", target_bir_lowering=False)
    S = Sched(nc)
    nseq = len(seq_lens)
    SMAX = max(seq_lens)
    lay, NS = smalls_layout(depth)

    x_in = [nc.dram_tensor("x%d" % s, [seq_lens[s], D], F32, kind="ExternalInput").ap() for s in range(nseq)]
    c_in = [nc.dram_tensor("c%d" % s, [128, KC], F32, kind="ExternalInput").ap() for s in range(nseq)]
    wf = nc.dram_tensor("wf", [depth, 128, TOT], F32, kind="ExternalInput").ap()
    wada = nc.dram_tensor("w_ada", [depth, D, 3 * D], F32, kind="ExternalInput").ap()
    bada = nc.dram_tensor("b_ada", [1, depth * 3 * D], F32, kind="ExternalInput").ap()
    smalls_d = nc.dram_tensor("smalls", [128, NS], F32, kind="ExternalInput").ap()
    fgain_d = nc.dram_tensor("fgain", [128, D], F32, kind="ExternalInput").ap()
    y_out = [nc.dram_tensor("y%d" % s, [seq_lens[s], D], F32, kind="ExternalOutput").ap() for s in range(nseq)]

    IK = "ExternalOutput" if debug else "Internal"
    wbf = nc.dram_tensor("wbf", [depth, 128, TOT], BF, kind="Internal").ap()
    cs_d = nc.dram_tensor("cs_d", [128, 2, SMAX], F32, kind=IK).ap()
    xs = [[nc.dram_tensor("xs%d_%d" % (s, b), [seq_lens[s], D], F32, kind="Internal").ap() for b in range(2)]
          for s in range(nseq)]
    kt_d = [nc.dram_tensor("kt%d" % s, [128, NKV, seq_lens[s]], BF, kind=IK).ap() for s in range(nseq)]
    v_d = [nc.dram_tensor("v%d" % s, [seq_lens[s], 256], BF, kind=IK).ap() for s in range(nseq)]
    xc_d = [nc.dram_tensor("xc%d" % s, [128, NG, seq_lens[s]], F32, kind=IK).ap() for s in range(nseq)]
    xcb_d = [nc.dram_tensor("xcbd%d" % s, [128, NG, seq_lens[s]], BF, kind="Internal").ap() for s in range(nseq)]
    hb_d = [nc.dram_tensor("hb%d" % s, [128, NG, seq_lens[s]], F32, kind=IK).ap() for s in range(nseq)]

    SIM = bool(os.environ.get("K_SIM"))
    CFG = dict(NW=8, NT=13, NL=6, NXB=2, NPT=2, NKO=2, NXR=4, NXA=3, LA=5, NMM=8, NTP=0, NBQ=1, NBA=1, NBR=1, NBM=1)
    for kv_ in os.environ.get("K_CFG", "").split(","):
        if "=" in kv_:
            CFG[kv_.split("=")[0]] = int(kv_.split("=")[1])
    sb_state = {"first": None}

    class _Dummy:
        def __getitem__(self, k):
            return self

        def __getattr__(self, k):
            return lambda *a, **kw: self

    def sb(name, shape, dt):
        if SIM:
            return _Dummy()
        return nc.alloc_sbuf_tensor(name, shape, dt)

    xa = [sb("xa%d" % i, [128, D], F32) for i in range(CFG["NXA"])]
    hTs = [[sb("hT%d_%d" % (q_, i), [128, KC, T], BF) for i in range(2)] for q_ in range(nseq)]
    NW = CFG["NW"]
    wring = [sb("wr%d" % i, [128, 1536], BF) for i in range(NW)]
    ident = sb("ident", [128, 128], F32)
    onesf = sb("onesf", [128, 128], F32)
    onesb = sb("onesb", [128, 128], BF)
    smalls = sb("smalls_sb", [128, NS], F32)
    cneg = sb("cneg", [128, depth * 24], F32)
    halfb = sb("halfb", [128, depth * 64], F32)
    NT = CFG["NT"]
    tmp = [sb("tmp%d" % i, [128, T], F32) for i in range(NT)]
    NL = CFG["NL"]
    ldb = [sb("ld%d" % i, [128, T], F32) for i in range(NL)]
    kout = [sb("kout%d" % i, [128, T], BF) for i in range(2)]
    vout = [sb("vout%d" % i, [128, 256], BF) for i in range(2)]
    xb = [sb("xb%d" % i, [128, T + 3], F32) for i in range(CFG["NXB"])]
    xcb = [sb("xcb%d" % i, [128, T], BF) for i in range(CFG["NXB"])]
    css = [sb("cs%d" % q_, [128, 2, T], F32) for q_ in range(nseq)]
    ktw = sb("ktw", [128, NKV, 768], BF)
    vw = sb("vw", [128, 6, 256], BF)
    qTs = [sb("qT%d" % i, [128, NH, T], BF) for i in range(CFG["NBQ"])]
    ags = [sb("ag%d" % i, [128, NH, T], BF) for i in range(CFG["NBQ"])]
    PT = [sb("PT%d" % i, [128, 3, T], BF) for i in range(CFG["NPT"])]
    attnTs = [sb("attnT%d" % i, [128, NH, T], BF) for i in range(CFG["NBA"])]
    rnnTs = [sb("rnnT%d" % i, [128, NG, T], BF) for i in range(CFG["NBR"])]
    mixTs = [sb("mixT%d" % i, [128, KC, T], BF) for i in range(CFG["NBM"])]
    gate_bcs = [sb("gate_bc%d" % q_, [128, D], F32) for q_ in range(nseq)]
    esinks = [sb("esink%d" % q_, [128, NH], F32) for q_ in range(nseq)]
    maskL = sb("maskL", [128, 4, 128], BF)
    maskR = sb("maskR", [128, 4, 128], BF)
    xrbig = sb("xrbig", [128, 4, T], F32)
    xr = [xrbig[:, i, :] for i in range(4)]
    stat = sb("stat", [128, 24], F32)
    carry_xs = [sb("carry_x%d" % q_, [128, NG], F32) for q_ in range(nseq)]
    carry_hs = [sb("carry_h%d" % q_, [128, NG], F32) for q_ in range(nseq)]
    sc_c = sb("sc_c", [128, KC], F32)
    sg_c = sb("sg_c", [128, KC], F32)
    modcs = [sb("modc%d" % q_, [128, 16], F32) for q_ in range(nseq)]
    sgn = sb("sgn", [128, 1], F32)
    ln2c = sb("ln2c", [128, 1], F32)
    invf = sb("invf", [128, 1], F32)
    one11 = sb("one11", [1, 1], F32)
    wadab = xrbig[:, 0:2, :].rearrange("p a (b c) -> p (a b) c", c=128)
    modrow = xrbig[0:1, 2:4, :].rearrange("p a b -> p (a b)")
    wstg = [sb("wstg%d" % i, [128, 1024], BF) for i in range(2)]
    junk = wstg[0]

    tp = [nc.alloc_psum_tensor("tp%d" % i, [128, 4, 128], F32) for i in range(CFG["NTP"])]
    NMM = CFG["NMM"]
    mmb = [(None if (SIM and i >= 8 - CFG["NTP"]) else nc.alloc_psum_tensor("mm%d" % i, [128, T], F32)) for i in range(CFG["NMM"])]
    st = dict(mm=0, tmp=0, ld=0, w=0)

    pending = []

    def defer(fn, reads, writes, key):
        pending.append((fn, reads, writes, key))

    def guard(key):
        for _, r, _, _ in pending:
            if key in r:
                flush()
                return

    def flush():
        for fn, r, w, k in pending:
            S.dma(fn, reads=r, writes=w, key=k)
        pending.clear()

    def next_mm():
        i = st["mm"] % NMM
        st["mm"] += 1
        return mmb[i], ("mm", i)

    def next_tmp():
        i = st["tmp"] % NT
        st["tmp"] += 1
        return tmp[i], ("tmp", i)

    def next_ld():
        i = st["ld"] % NL
        st["ld"] += 1
        guard(("ld", i))
        return ldb[i], ("ld", i)

    ws = dict(used=0)

    def w_get(l, name, sub=0, e_=None):
        i = ws["used"]
        ws["used"] += 1
        off, e = CAT[name]
        off += sub
        if e_ is not None:
            e = e_
        slot = i % NW
        S.dma(lambda en, slot=slot, l=l, off=off, e=e: en.dma_start(out=wring[slot][:, 0:e], in_=wbf[l, :, off:off + e]),
              reads=[("wbf", l, pc_) for pc_ in range(off // 1024, (off + e - 1) // 1024 + 1)], writes=[("w", slot)],
              key=("w", slot), nbytes=256 * e)
        return wring[slot], ("w", slot)

    S.dma(lambda e: e.dma_start(out=smalls[:, :], in_=smalls_d), writes=["smalls"], key="smalls")
    S.op("pool", lambda e: e.memset(onesf[:, :], 1.0), writes=["onesf"])
    S.op("pool", lambda e: e.memset(onesb[:, :], 2.0), writes=["onesb"])
    S.op("pool", lambda e: e.memset(one11[:, :], 1.0), writes=["one11"])
    S.op("pool", lambda e: e.affine_select(out=ident[:, :], in_=onesf[:, :], pattern=[[-1, 128]], compare_op=ALU.is_equal,
                                           fill=0.0, base=0, channel_multiplier=1), reads=["onesf"], writes=["ident"])
    S.op("pool", lambda e: e.memset(maskL[:, :, :], 1.0), writes=["maskL"])
    S.op("pool", lambda e: e.memset(maskR[:, :, :], 1.0), writes=["maskR"])
    S.op("pool", lambda e: e.affine_select(out=maskL[:, :, :], in_=maskL[:, :, :], pattern=[[0, 4], [-1, 128]],
                                           compare_op=ALU.is_ge, fill=0.0, base=0, channel_multiplier=1),
         reads=["maskL"], writes=["maskL"])
    S.op("pool", lambda e: e.affine_select(out=maskR[:, :, :], in_=maskR[:, :, :], pattern=[[0, 4], [1, 128]],
                                           compare_op=ALU.is_ge, fill=0.0, base=0, channel_multiplier=-1),
         reads=["maskR"], writes=["maskR"])
    S.op("pool", lambda e: e.memset(ln2c[:, :], math.log(2.0)), writes=["ln2c"])
    S.op("pool", lambda e: e.memset(sgn[0:64, :], -1.0), writes=["sgn"])
    S.op("pool", lambda e: e.memset(sgn[64:128, :], 1.0), reads=["sgn"], writes=["sgn"])
    lo, lw = lay["lam"]
    S.op("act:exp", lambda e: e.activation(out=cneg[:, :], in_=smalls[:, lo:lo + depth * lw], func=AF.Exp, scale=-1.0),
         reads=["smalls"], writes=["cneg"])
    S.op("act:ln", lambda e: e.activation(out=cneg[:, :], in_=cneg[:, :], func=AF.Ln, bias=1.0, scale=1.0),
         reads=["cneg"], writes=["cneg"])
    S.op("dve", lambda e: e.tensor_scalar(out=cneg[:, :], in0=cneg[:, :], scalar1=-4.0, scalar2=None, op0=ALU.mult),
         reads=["cneg"], writes=["cneg"])
    hb0 = lay["ba"][0]
    hb1 = lay["bx"][0] + depth * lay["bx"][1]
    assert lay["bx"][0] == lay["ba"][0] + depth * lay["ba"][1]
    S.op("dve", lambda e: e.tensor_scalar(out=halfb[:, 0:hb1 - hb0], in0=smalls[:, hb0:hb1], scalar1=0.5, scalar2=None, op0=ALU.mult),
         reads=["smalls"], writes=["halfb"])
    bm0, bmw_ = lay["bm"]
    S.op("dve", lambda e: e.tensor_scalar(out=halfb[:, hb1 - hb0:hb1 - hb0 + depth * bmw_], in0=smalls[:, bm0:bm0 + depth * bmw_],
                                          scalar1=0.5, scalar2=None, op0=ALU.mult),
         reads=["smalls", "halfb"], writes=["halfb"])
    S.op("pool", lambda e: e.iota(invf[0:64, :], pattern=[[0, 1]], base=0, channel_multiplier=1,
                                  allow_small_or_imprecise_dtypes=True), writes=["invf"])
    S.op("pool", lambda e: e.iota(invf[64:128, :], pattern=[[0, 1]], base=0, channel_multiplier=1,
                                  allow_small_or_imprecise_dtypes=True), reads=["invf"], writes=["invf"])
    S.op("act:exp", lambda e: e.activation(out=invf[:, :], in_=invf[:, :], func=AF.Exp, scale=-math.log(10000.0) / 64.0),
         reads=["invf"], writes=["invf"])
    I32 = mybir.dt.int32
    for pc in reversed(range(SMAX // T)):
        pos, kpos = next_tmp()
        S.op("pool", lambda e, pos=pos, pc=pc: e.iota(pos[:, :], pattern=[[1, T]], base=pc * T, channel_multiplier=0,
                                                      allow_small_or_imprecise_dtypes=True), writes=[kpos])
        S.op("dve", lambda e, pos=pos: e.tensor_scalar(out=pos[:, :], in0=pos[:, :], scalar1=invf[:, 0:1], scalar2=None,
                                                       op0=ALU.mult), reads=[kpos, "invf"], writes=[kpos])
        for which in range(2):
            a2, ka2 = next_tmp()
            ki_t, kki = next_tmp()
            kq, kkq = next_tmp()
            ki32 = ki_t[:, :].bitcast(I32)
            shift = math.pi / 2 if which == 0 else 0.0
            S.op("dve", lambda e, a2=a2, pos=pos, shift=shift: e.tensor_scalar(out=a2[:, :], in0=pos[:, :], scalar1=shift,
                                                                               scalar2=None, op0=ALU.add),
                 reads=[kpos], writes=[ka2])
            S.op("dve", lambda e, kq=kq, a2=a2: e.tensor_scalar(out=kq[:, :], in0=a2[:, :], scalar1=1.0 / TWO_PI,
                                                                scalar2=None, op0=ALU.mult), reads=[ka2], writes=[kkq])
            S.op("dve", lambda e, ki32=ki32, kq=kq: e.tensor_copy(out=ki32, in_=kq[:, :]), reads=[kkq], writes=[kki])
            S.op("dve", lambda e, ki32=ki32, kq=kq: e.tensor_copy(out=kq[:, :], in_=ki32), reads=[kki], writes=[kkq])
            for cst in (-C1, -C2):
                S.op("dve", lambda e, a2=a2, kq=kq, cst=cst: e.scalar_tensor_tensor(out=a2[:, :], in0=kq[:, :], scalar=cst,
                                                                                    in1=a2[:, :], op0=ALU.mult, op1=ALU.add),
                     reads=[kkq, ka2], writes=[ka2])
            for thr, cmp_, add in ((math.pi, ALU.is_gt, -TWO_PI), (-math.pi, ALU.is_lt, TWO_PI)):
                S.op("dve", lambda e, a2=a2, kq=kq, thr=thr, cmp_=cmp_: e.tensor_scalar(out=kq[:, :], in0=a2[:, :], scalar1=thr,
                                                                                        scalar2=None, op0=cmp_),
                     reads=[ka2], writes=[kkq])
                S.op("dve", lambda e, a2=a2, kq=kq, add=add: e.scalar_tensor_tensor(out=a2[:, :], in0=kq[:, :], scalar=add,
                                                                                    in1=a2[:, :], op0=ALU.mult, op1=ALU.add),
                     reads=[kkq, ka2], writes=[ka2])
            S.op("dve", lambda e, a2=a2: e.tensor_scalar(out=a2[:, :], in0=a2[:, :], scalar1=math.pi, scalar2=-math.pi,
                                                         op0=ALU.min, op1=ALU.max), reads=[ka2], writes=[ka2])
            S.op("act:sin", lambda e, a2=a2: e.activation(out=a2[:, :], in_=a2[:, :], func=AF.Sin), reads=[ka2], writes=[ka2])
            if which == 1:
                S.op("dve", lambda e, a2=a2: e.tensor_scalar(out=a2[:, :], in0=a2[:, :], scalar1=sgn[:, 0:1], scalar2=None,
                                                             op0=ALU.mult), reads=[ka2, "sgn"], writes=[ka2])
            S.dma(lambda e, a2=a2, which=which, pc=pc: e.dma_start(out=cs_d[:, which, pc * T:(pc + 1) * T], in_=a2[:, :]),
                  reads=[ka2], writes=[("cs_d", pc, which)], key=("cst", ka2[1]))
    PW = 1024
    npieces = (TOT + PW - 1) // PW
    for l in range(depth):
        for pc in range(npieces):
            c0 = pc * PW
            cw = min(PW, TOT - c0)
            q = l * npieces + pc
            sl = q % 3
            par = q % 2
            S.dma(lambda e, sl=sl, l=l, c0=c0, cw=cw: e.dma_start(out=xa[sl][:, 0:cw], in_=wf[l, :, c0:c0 + cw]),
                  writes=[("xa", sl)], key=("xa", sl))
            if par == 0:
                S.op("act", lambda e, par=par, sl=sl, cw=cw: e.activation(out=wstg[par][:, 0:cw], in_=xa[sl][:, 0:cw], func=AF.Copy),
                     reads=[("xa", sl)], writes=[("wstg", par)])
            else:
                S.op("pool", lambda e, par=par, sl=sl, cw=cw: e.tensor_copy(out=wstg[par][:, 0:cw], in_=xa[sl][:, 0:cw]),
                     reads=[("xa", sl)], writes=[("wstg", par)])
            S.dma(lambda e, par=par, l=l, c0=c0, cw=cw: e.dma_start(out=wbf[l, :, c0:c0 + cw], in_=wstg[par][:, 0:cw]),
                  reads=[("wstg", par)], writes=[("wbf", l, pc)], key=("wst", par))

    def stage_A(s, xsrc, i, hs):
        hT, modc = hTs[s], modcs[s]
        for t in range(NB):
            gt = i * NB + t
            sl = st.setdefault("xa_i", 0) % 3
            st["xa_i"] += 1
            guard(("xa", sl))
            col = st.setdefault("st_i", 0) % 8
            st["st_i"] += 1
            S.dma(lambda e, sl=sl, gt=gt: e.dma_start(out=xa[sl][:, :], in_=xsrc[0][gt * 128:(gt + 1) * 128, :]),
                  reads=[(xsrc[1], gt, c2) for c2 in range(2)], writes=[("xa", sl)], key=("xa", sl))
            S.op("act", lambda e, sl=sl, col=col: e.activation(out=junk[:, :], in_=xa[sl][:, :], func=AF.Square,
                                                               accum_out=stat[:, col:col + 1]),
                 reads=[("xa", sl)], writes=[("wstg", 0), ("ss", col)], cost=1250.0)
            S.op("act:sqrt", lambda e, col=col: e.activation(out=stat[:, 8 + col:9 + col], in_=stat[:, col:col + 1], func=AF.Sqrt,
                                                        scale=1.0 / D, bias=EPS), reads=[("ss", col)], writes=[("sd", col)], cost=250.0)
            S.op("dve", lambda e, col=col: e.reciprocal(out=stat[:, 16 + col:17 + col], in_=stat[:, 8 + col:9 + col]),
                 reads=[("sd", col)], writes=[("rs", col)], cost=120.0)
            S.op("pool", lambda e, sl=sl, col=col: e.tensor_scalar(out=xa[sl][:, :], in0=xa[sl][:, :],
                                                                   scalar1=stat[:, 16 + col:17 + col], scalar2=1.0,
                                                                   op0=ALU.mult, op1=ALU.mult),
                 reads=[("xa", sl), ("rs", col)], writes=[("xa", sl)], cost=2500.0)
            for half in range(2):
                if CFG["NTP"] > 0:
                    tpt, tpk = tp[half], ("tp", half)
                else:
                    tpt, tpk = next_mm()

                def tr(e, sl=sl, half=half, tpt=tpt):
                    ins = None
                    for kk in range(4):
                        kc = half * 4 + kk
                        ins = e.transpose(out=tpt[:, kk * 128:(kk + 1) * 128] if CFG["NTP"] == 0 else tpt[:, kk, :],
                                          in_=xa[sl][:, kc * 128:(kc + 1) * 128], identity=ident[:, :])
                    return ins
                S.op("pe", tr, reads=[("xa", sl), "ident"], writes=[tpk], cost=560.0)
                for kk in range(4):
                    kc = half * 4 + kk
                    S.op("act", lambda e, half=half, kk=kk, kc=kc, t=t, tpt=tpt: e.activation(
                        out=hT[hs][:, kc, t * 128:(t + 1) * 128],
                        in_=tpt[:, kk * 128:(kk + 1) * 128] if CFG["NTP"] == 0 else tpt[:, kk, :], func=AF.Identity,
                        scale=modc[:, 8 + kc:9 + kc], bias=modc[:, kc:kc + 1]),
                         reads=[tpk, ("modc", s)], writes=[("hT", s, hs, t, kc)], cost=370.0)

    def hT_keys(s, hs, ts=range(NB)):
        return [("hT", s, hs, t, kc) for t in ts for kc in range(KC)]

    def proj_mm(s, wt, wk, hs, ncols=T, c0=0, woff=0):
        ps, pk = next_mm()
        hT = hTs[s]

        def f(e):
            ins = None
            for kc in range(KC):
                ins = e.matmul(out=ps[:, 0:ncols], lhsT=wt[:, woff + kc * 128: woff + (kc + 1) * 128],
                               rhs=hT[hs][:, kc, c0:c0 + ncols], start=(kc == 0), stop=(kc == KC - 1))
            return ins
        ts = sorted(set([c0 // 128, (c0 + ncols - 1) // 128])) if ncols < T else range(NB)
        S.op("pe", f, reads=[wk] + hT_keys(s, hs, ts), writes=[pk], cost=(2150.0 if ncols >= T else 600.0))
        return ps, pk

    def setup_mod(s, l):
        modc, gate_bc, esink = modcs[s], gate_bcs[s], esinks[s]
        for k_ in range(4):
            guard(("xr", k_))
        S.dma(lambda e: e.dma_start(out=sc_c[:, :], in_=c_in[s]), writes=["sc_c"], key="cin")
        S.op("act:sig", lambda e: e.activation(out=sg_c[:, :], in_=sc_c[:, :], func=AF.Sigmoid), reads=["sc_c"], writes=["sg_c"])
        S.op("dve", lambda e: e.tensor_tensor(out=sc_c[:, :], in0=sc_c[:, :], in1=sg_c[:, :], op=ALU.mult),
             reads=["sc_c", "sg_c"], writes=["sc_c"])
        wv = wada[l].rearrange("(kc p) c -> p kc c", p=128)
        no, nw = lay["ng"]
        for part in range(3):
            S.dma(lambda e, part=part: e.dma_start(out=modrow[:, :], in_=bada[:, l * 3 * D + part * D: l * 3 * D + (part + 1) * D]),
                  writes=[("xr", 2), ("xr", 3)], key="bada")
            for cq in range(8):
                cg = part * 8 + cq
                S.dma(lambda e, cg=cg: e.dma_start(out=wadab[:, :, :], in_=wv[:, :, cg * 128:(cg + 1) * 128]),
                      writes=[("xr", 0), ("xr", 1)], key="wada")
                ps, pk = next_mm()

                def f(e, ps=ps):
                    ins = None
                    for kc in range(KC):
                        ins = e.matmul(out=ps[0:1, 0:128], lhsT=sc_c[:, kc:kc + 1], rhs=wadab[:, kc, :], start=(kc == 0),
                                       stop=(kc == KC - 1))
                    return ins
                S.op("pe", f, reads=["sc_c", ("xr", 0), ("xr", 1)], writes=[pk])
                S.op("dve", lambda e, ps=ps, cq=cq: e.tensor_tensor(out=modrow[0:1, cq * 128:(cq + 1) * 128], in0=ps[0:1, 0:128],
                                                                    in1=modrow[0:1, cq * 128:(cq + 1) * 128], op=ALU.add),
                     reads=[pk, ("xr", 2), ("xr", 3)], writes=[("xr", 2), ("xr", 3)])
            if part < 2:
                ps, pk = next_mm()

                def f2(e, ps=ps):
                    ins = None
                    for j in range(8):
                        ins = e.matmul(out=ps[:, j:j + 1], lhsT=modrow[0:1, j * 128:(j + 1) * 128], rhs=one11[0:1, 0:1],
                                       start=True, stop=True)
                    return ins
                S.op("pe", f2, reads=[("xr", 2), ("xr", 3), "one11"], writes=[pk])
                if part == 0:
                    S.op("dve", lambda e, ps=ps: e.tensor_copy(out=modc[:, 0:8], in_=ps[:, 0:8]), reads=[pk, ("modc", s)], writes=[("modc", s)])
                else:
                    S.op("dve", lambda e, ps=ps: e.scalar_tensor_tensor(out=modc[:, 8:16], in0=ps[:, 0:8], scalar=1.0,
                                                                        in1=smalls[:, no + l * nw: no + (l + 1) * nw],
                                                                        op0=ALU.add, op1=ALU.mult),
                         reads=[pk, "smalls", ("modc", s)], writes=[("modc", s)])
            else:
                for half in range(2):
                    ps, pk = next_mm()
                    S.op("pe", lambda e, ps=ps, half=half: e.matmul(out=ps[:, :], lhsT=onesf[0:1, :],
                                                                    rhs=modrow[0:1, half * T:(half + 1) * T],
                                                                    start=True, stop=True), reads=["onesf", ("xr", 2), ("xr", 3)], writes=[pk])
                    S.op("dve", lambda e, ps=ps, half=half: e.tensor_scalar(out=gate_bc[:, half * T:(half + 1) * T], in0=ps[:, :], scalar1=0.5,
                                                                            scalar2=None, op0=ALU.mult),
                         reads=[pk, ("gate_bc", s)], writes=[("gate_bc", s)])
        so, sw = lay["sink"]
        S.op("act:exp", lambda e: e.activation(out=esink[:, :], in_=smalls[:, so + l * sw: so + (l + 1) * sw], func=AF.Exp,
                                               bias=ln2c[:, 0:1]),
             reads=["smalls", "ln2c"], writes=[("esink", s)], cost=250.0)

    def gates_and_scan(s, l, d, g, xc_t, kxc, xcb_t, kxcb, reverse, first):
        carry_h = carry_hs[s]
        gimg, gk = w_get(l, "%s%d" % ("GB" if d == 1 else "GF", g // 6), sub=(g % 6) * 256, e_=256)
        gg = 0
        psA, pkA = next_mm()
        S.op("pe", lambda e: e.matmul(out=psA[:, :], lhsT=gimg[:, (gg * 2) * 128:(gg * 2 + 1) * 128], rhs=xcb_t[:, :],
                                      start=True, stop=True), reads=[gk, kxcb], writes=[pkA], cost=300.0)
        psX, pkX = next_mm()
        S.op("pe", lambda e: e.matmul(out=psX[:, :], lhsT=gimg[:, (gg * 2 + 1) * 128:(gg * 2 + 2) * 128], rhs=xcb_t[:, :],
                                      start=True, stop=True), reads=[gk, kxcb], writes=[pkX], cost=300.0)
        bo, bw = lay["ba"]
        xo, xw = lay["bx"]
        ci = l * 24 + d * 12 + g
        ra, kra = next_tmp()
        ib, kib = next_tmp()
        sq, ksq = next_tmp()
        hci = ci
        hxi = depth * 24 + ci
        S.op("act:exp", lambda e: e.activation(out=ra[:, :], in_=psA[:, :], func=AF.Tanh, scale=0.5,
                                               bias=halfb[:, hci: hci + 1]), reads=[pkA, "halfb"], writes=[kra])
        S.op("act:exp", lambda e: e.activation(out=ib[:, :], in_=psX[:, :], func=AF.Tanh, scale=0.5,
                                               bias=halfb[:, hxi: hxi + 1]), reads=[pkX, "halfb"], writes=[kib])
        S.op("act:exp", lambda e: e.activation(out=ra[:, :], in_=ra[:, :], func=AF.Exp, scale=cneg[:, ci:ci + 1],
                                               bias=cneg[:, ci:ci + 1]), reads=[kra, "cneg"], writes=[kra])
        S.op("act", lambda e: e.activation(out=sq[:, :], in_=ra[:, :], func=AF.Square), reads=[kra], writes=[ksq])
        S.op("act:sqrt", lambda e: e.activation(out=sq[:, :], in_=sq[:, :], func=AF.Sqrt, scale=-0.25, bias=0.25),
             reads=[ksq], writes=[ksq])
        S.op("dve", lambda e: e.scalar_tensor_tensor(out=ib[:, :], in0=ib[:, :], scalar=1.0, in1=xc_t[:, :],
                                                     op0=ALU.add, op1=ALU.mult),
             reads=[kib, kxc], writes=[kib], cost=1100.0)
        S.op("pool", lambda e: e.tensor_tensor(out=ib[:, :], in0=ib[:, :], in1=sq[:, :], op=ALU.mult),
             reads=[kib, ksq], writes=[kib])
        h, kh = next_ld()
        if reverse:
            S.op("dve", lambda e: e.tensor_tensor_scan(out=h[:, ::-1], data0=ra[:, ::-1], data1=ib[:, ::-1],
                                                       initial=(0.0 if first else carry_h[:, g:g + 1]),
                                                       op0=ALU.mult, op1=ALU.add),
                 reads=[kra, kib, ("ch", s, g)], writes=[kh], cost=1250.0)
            S.op("act", lambda e: e.activation(out=carry_h[:, g:g + 1], in_=h[:, 0:1], func=AF.Copy), reads=[kh],
                 writes=[("ch", s, g)], cost=250.0)
        else:
            S.op("dve", lambda e: e.tensor_tensor_scan(out=h[:, :], data0=ra[:, :], data1=ib[:, :],
                                                       initial=(0.0 if first else carry_h[:, g:g + 1]),
                                                       op0=ALU.mult, op1=ALU.add),
                 reads=[kra, kib, ("ch", s, g)], writes=[kh], cost=1250.0)
            S.op("act", lambda e: e.activation(out=carry_h[:, g:g + 1], in_=h[:, T - 1:T], func=AF.Copy), reads=[kh],
                 writes=[("ch", s, g)], cost=250.0)
        return h, kh

    def pass1(s, l, xsrc):
        hT, cs, carry_x = hTs[s], css[s], carry_xs[s]
        L = seq_lens[s]
        nch = L // T
        stage_A(s, xsrc, nch - 1, (nch - 1) % 2)
        co, cwid = lay["cw"]
        cbo, cbw = lay["cb"]
        def chunk(i):
            hs = i % 2
            first = (i == nch - 1)
            if i > 0:
                stage_A(s, xsrc, i - 1, (i - 1) % 2)
            S.dma(lambda e, i=i: e.dma_start(out=cs[:, :, :], in_=cs_d[:, :, i * T:(i + 1) * T]),
                  reads=[("cs_d", i, 0), ("cs_d", i, 1)], writes=[("cs", s)], key=("cs", s))
            flush()
            for j in range(NKV):
                wt, wk = w_get(l, "K%d" % j)
                ps1, pk1 = proj_mm(s, wt, wk, hs)
                t1, kt1 = next_tmp()
                t2, kt2 = next_tmp()
                S.op("dve", lambda e, t1=t1, ps1=ps1: e.tensor_tensor(out=t1[:, :], in0=ps1[:, :], in1=cs[:, 0, :], op=ALU.mult),
                     reads=[pk1, ("cs", s)], writes=[kt1])
                S.op("dve", lambda e, t2=t2, ps1=ps1: e.tensor_tensor(out=t2[0:64, :], in0=ps1[64:128, :], in1=cs[0:64, 1, :], op=ALU.mult),
                     reads=[pk1, ("cs", s)], writes=[kt2])
                S.op("dve", lambda e, t2=t2, ps1=ps1: e.tensor_tensor(out=t2[64:128, :], in0=ps1[0:64, :], in1=cs[64:128, 1, :], op=ALU.mult),
                     reads=[pk1, ("cs", s), kt2], writes=[kt2])
                ko = st.setdefault("ko", 0) % 2
                st["ko"] += 1
                guard(("kout", ko))
                S.op("pool", lambda e, t1=t1, t2=t2, ko=ko: e.tensor_tensor(out=kout[ko][:, :], in0=t1[:, :], in1=t2[:, :], op=ALU.add),
                     reads=[kt1, kt2], writes=[("kout", ko)])
                defer(lambda e, ko=ko, j=j, i=i: e.dma_start(out=kt_d[s][:, j, i * T:(i + 1) * T], in_=kout[ko][:, :]),
                      [("kout", ko)], [("kt_d", s, i, j)], ("kst", ko))
                yield F1
            wt, wk = w_get(l, "V", sub=0, e_=1024)
            wtb, wkb = w_get(l, "V", sub=1024, e_=1024)
            for t in range(NB):
                ps, pk = next_mm()

                def fv(e, ps=ps, t=t, wt=wt, wtb=wtb):
                    ins = None
                    for kc in range(KC):
                        w_ = wt if kc < 4 else wtb
                        ins = e.matmul(out=ps[:, 0:256], lhsT=hT[hs][:, kc, t * 128:(t + 1) * 128],
                                       rhs=w_[:, (kc % 4) * 256:(kc % 4 + 1) * 256], start=(kc == 0), stop=(kc == KC - 1))
                    return ins
                S.op("pe", fv, reads=[wk, wkb] + hT_keys(s, hs, [t]), writes=[pk], cost=1400.0)
                vo = st.setdefault("vo", 0) % 2
                st["vo"] += 1
                guard(("vout", vo))
                S.op("act", lambda e, ps=ps, vo=vo: e.activation(out=vout[vo][:, :], in_=ps[:, 0:256], func=AF.Copy),
                     reads=[pk], writes=[("vout", vo)])
                defer(lambda e, vo=vo, t=t, i=i: e.dma_start(out=v_d[s][(i * NB + t) * 128:(i * NB + t + 1) * 128, :], in_=vout[vo][:, :]),
                      [("vout", vo)], [("v_d", s, i, t)], ("vst", vo))
                if t % 2 == 1:
                    flush()
            yield F1
            for g in range(NG):
                wt, wk = w_get(l, "RX%d" % g)
                ps, pk = proj_mm(s, wt, wk, hs)
                xc_t, kxc = next_ld()
                cwb = co + l * cwid

                def wcol(tap, g=g, cwb=cwb):
                    return smalls[:, cwb + tap * 12 + g: cwb + tap * 12 + g + 1]
                S.op("dve", lambda e, xc_t=xc_t, ps=ps, g=g, wcol=wcol: e.tensor_scalar(
                    out=xc_t[:, :], in0=ps[:, :], scalar1=wcol(2), scalar2=smalls[:, cbo + l * cbw + g: cbo + l * cbw + g + 1],
                    op0=ALU.mult, op1=ALU.add), reads=[pk, "smalls"], writes=[kxc], cost=720.0)
                for tap, (o0, o1, i0, i1) in ((0, (2, T, 0, T - 2)), (1, (1, T, 0, T - 1)), (3, (0, T - 1, 1, T))):
                    S.op("dve", lambda e, xc_t=xc_t, ps=ps, tap=tap, o0=o0, o1=o1, i0=i0, i1=i1, wcol=wcol: e.scalar_tensor_tensor(
                        out=xc_t[:, o0:o1], in0=ps[:, i0:i1], scalar=wcol(tap), in1=xc_t[:, o0:o1], op0=ALU.mult, op1=ALU.add),
                         reads=[pk, kxc, "smalls"], writes=[kxc], cost=740.0)
                if i > 0:
                    psh, pkh = proj_mm(s, wt, wk, (i - 1) % 2, ncols=2, c0=T - 2)
                    S.op("dve", lambda e, xc_t=xc_t, psh=psh, wcol=wcol: e.scalar_tensor_tensor(
                        out=xc_t[:, 0:2], in0=psh[:, 0:2], scalar=wcol(0), in1=xc_t[:, 0:2], op0=ALU.mult, op1=ALU.add),
                         reads=[pkh, kxc, "smalls"], writes=[kxc], cost=160.0)
                    S.op("dve", lambda e, xc_t=xc_t, psh=psh, wcol=wcol: e.scalar_tensor_tensor(
                        out=xc_t[:, 0:1], in0=psh[:, 1:2], scalar=wcol(1), in1=xc_t[:, 0:1], op0=ALU.mult, op1=ALU.add),
                         reads=[pkh, kxc, "smalls"], writes=[kxc], cost=160.0)
                if not first:
                    S.op("dve", lambda e, xc_t=xc_t, g=g, wcol=wcol: e.scalar_tensor_tensor(
                        out=xc_t[:, T - 1:T], in0=carry_x[:, g:g + 1], scalar=wcol(3), in1=xc_t[:, T - 1:T], op0=ALU.mult, op1=ALU.add),
                         reads=[("cx", s, g), kxc, "smalls"], writes=[kxc], cost=160.0)
                S.op("act", lambda e, ps=ps, g=g: e.activation(out=carry_x[:, g:g + 1], in_=ps[:, 0:1], func=AF.Copy),
                     reads=[pk, kxc], writes=[("cx", s, g)], cost=250.0)
                xi2 = st.setdefault("xcb", 0) % CFG["NXB"]
                st["xcb"] += 1
                guard(("xcb", xi2))
                S.op("pool", lambda e, xc_t=xc_t, xi2=xi2: e.tensor_copy(out=xcb[xi2][:, :], in_=xc_t[:, :]),
                     reads=[kxc], writes=[("xcb", xi2)])
                defer(lambda e, xc_t=xc_t, g=g, i=i: e.dma_start(out=xc_d[s][:, g, i * T:(i + 1) * T], in_=xc_t[:, :]),
                      [kxc], [("xc_d", s, i, g)], ("xcst", kxc[1]))
                defer(lambda e, xi2=xi2, g=g, i=i: e.dma_start(out=xcb_d[s][:, g, i * T:(i + 1) * T], in_=xcb[xi2][:, :]),
                      [("xcb", xi2)], [("xcb_d", s, i, g)], ("xcbst", xi2))
                h, kh = gates_and_scan(s, l, 1, g, xc_t, kxc, xcb[xi2], ("xcb", xi2), True, first)
                defer(lambda e, h=h, g=g, i=i: e.dma_start(out=hb_d[s][:, g, i * T:(i + 1) * T], in_=h[:, :]),
                      [kh], [("hb_d", s, i, g)], ("hbst", kh[1]))
                if g % 2 == 1:
                    flush()
                yield F1
        for i in range(nch - 1, -1, -1):
            yield from chunk(i)
            yield "chunk_end"
        flush()

    def pass2(s, l, xsrc, xdst):
        hT, cs, gate_bc, esink = hTs[s], css[s], gate_bcs[s], esinks[s]
        L = seq_lens[s]
        nch = L // T
        nblk = L // 128
        stage_A(s, xsrc, 0, 0)
        isq = 1.0 / math.sqrt(128.0)
        bmo, bmw = lay["bm"]
        def chunk(i):
            yield "p2_begin"
            hs = i % 2
            first = (i == 0)
            cix = st.setdefault("cix", 0)
            st["cix"] += 1
            bq, ba_, br_, bm_ = cix % CFG["NBQ"], cix % CFG["NBA"], cix % CFG["NBR"], cix % CFG["NBM"]
            qT, ag, attnT, rnnT, mixT = qTs[bq], ags[bq], attnTs[ba_], rnnTs[br_], mixTs[bm_]
            lo_t = max(0, i * T - 128)
            hi_t = min(L, i * T + T + 128)
            woff = lo_t - (i * T - 128)
            wn = hi_t - lo_t
            kreads = [("kt_d", s, ii, j) for ii in (i - 1, i, i + 1) if 0 <= ii < nch for j in range(NKV)]
            vreads = [("v_d", s, ii, t) for ii in (i - 1, i, i + 1) if 0 <= ii < nch for t in range(NB)]
            S.dma(lambda e, lo_t=lo_t, hi_t=hi_t, woff=woff, wn=wn: e.dma_start(out=ktw[:, :, woff:woff + wn], in_=kt_d[s][:, :, lo_t:hi_t]),
                  reads=kreads, writes=["ktw"], key="ktw")
            b0 = lo_t // 128
            nbw = wn // 128
            S.dma(lambda e, lo_t=lo_t, hi_t=hi_t, woff=woff, nbw=nbw: e.dma_start(
                out=vw[:, woff // 128: woff // 128 + nbw, :], in_=v_d[s][lo_t:hi_t, :].rearrange("(b p) c -> p b c", p=128)),
                  reads=vreads, writes=["vw"], key="vw")
            S.dma(lambda e, i=i: e.dma_start(out=cs[:, :, :], in_=cs_d[:, :, i * T:(i + 1) * T]),
                  reads=[("cs_d", i, 0), ("cs_d", i, 1)], writes=[("cs", s)], key=("cs", s))
            if i + 1 < nch:
                stage_A(s, xsrc, i + 1, (i + 1) % 2)
            flush()
            for j in range(NH):
                wt, wk = w_get(l, "Q%d" % j)
                ps1, pk1 = proj_mm(s, wt, wk, hs)
                t1, kt1 = next_tmp()
                t2, kt2 = next_tmp()
                S.op("dve", lambda e, t1=t1, ps1=ps1: e.tensor_tensor(out=t1[:, :], in0=ps1[:, :], in1=cs[:, 0, :], op=ALU.mult),
                     reads=[pk1, ("cs", s)], writes=[kt1])
                S.op("dve", lambda e, t2=t2, ps1=ps1: e.tensor_tensor(out=t2[0:64, :], in0=ps1[64:128, :], in1=cs[0:64, 1, :], op=ALU.mult),
                     reads=[pk1, ("cs", s)], writes=[kt2])
                S.op("dve", lambda e, t2=t2, ps1=ps1: e.tensor_tensor(out=t2[64:128, :], in0=ps1[0:64, :], in1=cs[64:128, 1, :], op=ALU.mult),
                     reads=[pk1, ("cs", s), kt2], writes=[kt2])
                S.op("pool", lambda e, t1=t1, t2=t2, j=j: e.tensor_tensor(out=qT[:, j, :], in0=t1[:, :], in1=t2[:, :], op=ALU.add),
                     reads=[kt1, kt2], writes=[("qT", bq, j)])
                yield F2
            for j in range(NH):
                wt, wk = w_get(l, "AG%d" % j)
                ps, pk = proj_mm(s, wt, wk, hs)
                sg, ksg = next_tmp()
                S.op("act:exp", lambda e, sg=sg, ps=ps: e.activation(out=sg[:, :], in_=ps[:, :], func=AF.Tanh, scale=0.5), reads=[pk], writes=[ksg])
                S.op("dve", lambda e, sg=sg, ps=ps, j=j: e.scalar_tensor_tensor(out=ag[:, j, :], in0=sg[:, :], scalar=1.0, in1=ps[:, :],
                                                                                op0=ALU.add, op1=ALU.mult),
                     reads=[pk, ksg], writes=[("ag", bq, j)], cost=960.0)
                yield F2
            for n in range(NB):
                gb = i * NB + n
                for kv in range(NKV):
                    pi_ = st.setdefault("pt", 0) % CFG["NPT"]
                    st["pt"] += 1
                    ptt = PT[pi_]
                    kbs = [kb for kb in (0, 1, 2) if 0 <= gb + kb - 1 < nblk]
                    for kb in kbs:
                        wc = (n + kb) * 128
                        ps, pk = next_mm()
                        S.op("pe", lambda e, ps=ps, wc=wc, kv=kv, n=n: e.matmul(
                            out=ps[:, :], lhsT=ktw[:, kv, wc:wc + 128], rhs=qT[:, kv * 4:(kv + 1) * 4, n * 128:(n + 1) * 128],
                            start=True, stop=True), reads=["ktw"] + [("qT", bq, kv * 4 + h4) for h4 in range(4)], writes=[pk], cost=300.0)
                        S.op("act:exp", lambda e, ps=ps, ptt=ptt, kb=kb: e.activation(out=ptt[:, kb, :], in_=ps[:, :], func=AF.Exp, scale=isq),
                             reads=[pk], writes=[("PT", pi_, kb)])
                        if kb != 1:
                            mk = maskL if kb == 0 else maskR
                            S.op("pool", lambda e, ptt=ptt, kb=kb, mk=mk: e.tensor_tensor(
                                out=ptt[:, kb, :], in0=ptt[:, kb, :], in1=mk[:, :, :].rearrange("p a b -> p (a b)"), op=ALU.mult),
                                 reads=[("PT", pi_, kb), "maskL", "maskR"], writes=[("PT", pi_, kb)])
                    pso, pko = next_mm()

                    def fo(e, pso=pso, kbs=kbs, n=n, kv=kv, ptt=ptt):
                        ins = None
                        for q_, kb in enumerate(kbs):
                            ins = e.matmul(out=pso[:, :], lhsT=vw[:, n + kb, kv * 128:(kv + 1) * 128], rhs=ptt[:, kb, :],
                                           start=(q_ == 0), stop=(q_ == len(kbs) - 1))
                        return ins
                    S.op("pe", fo, reads=["vw"] + [("PT", pi_, kb) for kb in kbs], writes=[pko], cost=270.0 * len(kbs))
                    psd, pkd = next_mm()

                    def fd(e, psd=psd, kbs=kbs, ptt=ptt):
                        ins = None
                        for q_, kb in enumerate(kbs):
                            ins = e.matmul(out=psd[:, :], lhsT=onesb[:, :], rhs=ptt[:, kb, :],
                                           start=(q_ == 0), stop=(q_ == len(kbs) - 1))
                        return ins
                    S.op("pe", fd, reads=["onesb"] + [("PT", pi_, kb) for kb in kbs], writes=[pkd], cost=270.0 * len(kbs))
                    den, kden = next_tmp()
                    S.op("dve", lambda e, den=den, psd=psd, kv=kv: e.tensor_tensor(
                        out=den[:, :].rearrange("p (a b) -> p a b", a=4), in0=psd[:, :].rearrange("p (a b) -> p a b", a=4),
                        in1=esink[:, kv * 4:(kv + 1) * 4, None].broadcast_to([128, 4, 128]), op=ALU.add),
                         reads=[pkd, ("esink", s)], writes=[kden])
                    S.op("dve", lambda e, den=den: e.reciprocal(out=den[:, :], in_=den[:, :]), reads=[kden], writes=[kden], cost=1660.0)
                    S.op("pool", lambda e, den=den, kv=kv, n=n: e.tensor_tensor(
                        out=den[:, :].rearrange("p (a b) -> p a b", a=4), in0=den[:, :].rearrange("p (a b) -> p a b", a=4),
                        in1=ag[:, kv * 4:(kv + 1) * 4, n * 128:(n + 1) * 128], op=ALU.mult),
                         reads=[kden] + [("ag", bq, kv * 4 + h4) for h4 in range(4)], writes=[kden])
                    S.op("dve", lambda e, den=den, pso=pso, kv=kv, n=n: e.tensor_tensor(
                        out=attnT[:, kv * 4:(kv + 1) * 4, n * 128:(n + 1) * 128], in0=pso[:, :].rearrange("p (a b) -> p a b", a=4),
                        in1=den[:, :].rearrange("p (a b) -> p a b", a=4), op=ALU.mult),
                         reads=[pko, kden], writes=[("attnT", ba_, kv, n)])
                    yield F2
            def rnn_loads(g):
                xc_t, kxc = next_ld()
                hb_t, khb = next_ld()
                S.dma(lambda e, xc_t=xc_t, g=g: e.dma_start(out=xc_t[:, :], in_=xc_d[s][:, g, i * T:(i + 1) * T]),
                      reads=[("xc_d", s, i, g)], writes=[kxc], key=("xcl", kxc[1]))
                S.dma(lambda e, hb_t=hb_t, g=g: e.dma_start(out=hb_t[:, :], in_=hb_d[s][:, g, i * T:(i + 1) * T]),
                      reads=[("hb_d", s, i, g)], writes=[khb], key=("hbl", khb[1]))
                return xc_t, kxc, hb_t, khb
            for g in range(NG):
                xc_t, kxc, hb_t, khb = rnn_loads(g)
                xi2 = st.setdefault("xcb", 0) % CFG["NXB"]
                st["xcb"] += 1
                guard(("xcb", xi2))
                S.dma(lambda e, xi2=xi2, g=g: e.dma_start(out=xcb[xi2][:, :], in_=xcb_d[s][:, g, i * T:(i + 1) * T]),
                      reads=[("xcb_d", s, i, g)], writes=[("xcb", xi2)], key=("xcbl", xi2), nbytes=131072)
                wt, wk = w_get(l, "RG%d" % g)
                ps, pk = proj_mm(s, wt, wk, hs)
                sg, ksg = next_tmp()
                S.op("act:exp", lambda e, sg=sg, ps=ps: e.activation(out=sg[:, :], in_=ps[:, :], func=AF.Tanh, scale=0.5), reads=[pk], writes=[ksg])
                S.op("dve", lambda e, sg=sg, ps=ps: e.scalar_tensor_tensor(out=sg[:, :], in0=sg[:, :], scalar=1.0, in1=ps[:, :],
                                                                           op0=ALU.add, op1=ALU.mult),
                     reads=[pk, ksg], writes=[ksg], cost=960.0)
                h, kh = gates_and_scan(s, l, 0, g, xc_t, kxc, xcb[xi2], ("xcb", xi2), False, first)
                S.op("pool", lambda e, h=h, hb_t=hb_t: e.tensor_tensor(out=h[:, :], in0=h[:, :], in1=hb_t[:, :], op=ALU.add),
                     reads=[kh, khb], writes=[kh])
                S.op("dve", lambda e, h=h, sg=sg, g=g: e.scalar_tensor_tensor(out=rnnT[:, g, :], in0=h[:, :], scalar=0.5, in1=sg[:, :],
                                                                              op0=ALU.mult, op1=ALU.mult),
                     reads=[kh, ksg], writes=[("rnnT", br_, g)], cost=1100.0)
                yield F2
            for m in range(8):
                wa, wka = w_get(l, "MA%d" % m)
                psa, pka = proj_mm(s, wa, wka, hs)
                wr, wkr = w_get(l, "MR%d" % m)
                psr, pkr = proj_mm(s, wr, wkr, hs)
                ga, kga = next_tmp()
                gr, kgr = next_tmp()
                hbm = depth * 48 + l * 16
                S.op("act:exp", lambda e, ga=ga, psa=psa, m=m: e.activation(out=ga[:, :], in_=psa[:, :], func=AF.Tanh, scale=0.5,
                                                                        bias=halfb[:, hbm + m: hbm + m + 1]),
                     reads=[pka, "halfb"], writes=[kga])
                S.op("act:exp", lambda e, gr=gr, psr=psr, m=m: e.activation(out=gr[:, :], in_=psr[:, :], func=AF.Tanh, scale=0.5,
                                                                        bias=halfb[:, hbm + 8 + m: hbm + 9 + m]),
                     reads=[pkr, "halfb"], writes=[kgr])
                wp, wkp = w_get(l, "AP%d" % m)
                psp, pkp = next_mm()

                def fp(e, psp=psp, wp=wp):
                    ins = None
                    for kc in range(8):
                        ins = e.matmul(out=psp[:, :], lhsT=wp[:, kc * 128:(kc + 1) * 128], rhs=attnT[:, kc, :],
                                       start=(kc == 0), stop=(kc == 7))
                    return ins
                S.op("pe", fp, reads=[wkp] + [("attnT", ba_, kv, n) for kv in range(NKV) for n in range(NB)], writes=[pkp])
                wq, wkq = w_get(l, "RP%d" % m)
                psq, pkq = next_mm()

                def fq(e, psq=psq, wq=wq):
                    ins = None
                    for kc in range(NG):
                        ins = e.matmul(out=psq[:, :], lhsT=wq[:, kc * 128:(kc + 1) * 128], rhs=rnnT[:, kc, :],
                                       start=(kc == 0), stop=(kc == NG - 1))
                    return ins
                S.op("pe", fq, reads=[wkq] + [("rnnT", br_, g) for g in range(NG)], writes=[pkq], cost=3200.0)
                S.op("dve", lambda e, ga=ga, psp=psp: e.scalar_tensor_tensor(out=ga[:, :], in0=ga[:, :], scalar=1.0, in1=psp[:, :],
                                                                             op0=ALU.add, op1=ALU.mult),
                     reads=[pkp, kga], writes=[kga], cost=960.0)
                S.op("dve", lambda e, gr=gr, psq=psq: e.scalar_tensor_tensor(out=gr[:, :], in0=gr[:, :], scalar=1.0, in1=psq[:, :],
                                                                             op0=ALU.add, op1=ALU.mult),
                     reads=[pkq, kgr], writes=[kgr], cost=960.0)
                S.op("pool", lambda e, ga=ga, gr=gr, m=m: e.tensor_tensor(out=mixT[:, m, :], in0=ga[:, :], in1=gr[:, :], op=ALU.add),
                     reads=[kga, kgr], writes=[("mixT", bm_, m)])
                yield F2
            for ch in range(2):
                wos = [w_get(l, "WO%d%d" % (ch, q_ // 2), sub=(q_ % 2) * 1024, e_=1024) for q_ in range(4)]
                for t in range(NB):
                    gt = i * NB + t
                    xi = st.setdefault("xr", 0) % 4
                    st["xr"] += 1
                    guard(("xr", xi))
                    S.dma(lambda e, xi=xi, gt=gt, ch=ch: e.dma_start(out=xr[xi][:, :], in_=xsrc[0][gt * 128:(gt + 1) * 128, ch * T:(ch + 1) * T]),
                          reads=[(xsrc[1], gt, c2) for c2 in range(2)], writes=[("xr", xi)], key=("xr", xi))
                    ps, pk = next_mm()

                    def fw(e, ps=ps, t=t, wos=wos):
                        ins = None
                        for kc in range(KC):
                            wt_ = wos[kc // 2][0]
                            ins = e.matmul(out=ps[:, :], lhsT=mixT[:, kc, t * 128:(t + 1) * 128],
                                           rhs=wt_[:, (kc % 2) * 512:(kc % 2 + 1) * 512], start=(kc == 0), stop=(kc == KC - 1))
                        return ins
                    S.op("pe", fw, reads=[w_[1] for w_ in wos] + [("mixT", bm_, m) for m in range(8)], writes=[pk])
                    y, ky = next_tmp()
                    S.op("dve", lambda e, y=y, ps=ps, ch=ch: e.tensor_tensor(out=y[:, :], in0=ps[:, :], in1=gate_bc[:, ch * T:(ch + 1) * T],
                                                                           op=ALU.mult), reads=[pk, ("gate_bc", s)], writes=[ky])
                    S.op("pool", lambda e, y=y, xi=xi: e.tensor_tensor(out=xr[xi][:, :], in0=y[:, :], in1=xr[xi][:, :], op=ALU.add),
                         reads=[ky, ("xr", xi)], writes=[("xr", xi)])
                    defer(lambda e, xi=xi, gt=gt, ch=ch: e.dma_start(out=xdst[0][gt * 128:(gt + 1) * 128, ch * T:(ch + 1) * T], in_=xr[xi][:, :]),
                          [("xr", xi)], [(xdst[1], gt, ch)], ("xst", xi))
                    if t % 2 == 1:
                        flush()
                yield F2
        for i in range(nch):
            yield from chunk(i)
            yield "chunk_end"
        flush()

    def pass3(s, xsrc):
        fg_bc = gate_bcs[s]
        S.dma(lambda e: e.dma_start(out=fg_bc[:, :], in_=fgain_d), writes=[("gate_bc", s)], key=("fg", s))
        L = seq_lens[s]
        for gt in range(L // 128):
            sl = st.setdefault("xa_i", 0) % 3
            st["xa_i"] += 1
            guard(("xa", sl))
            col = st.setdefault("st_i", 0) % 8
            st["st_i"] += 1
            S.dma(lambda e, sl=sl, gt=gt: e.dma_start(out=xa[sl][:, :], in_=xsrc[0][gt * 128:(gt + 1) * 128, :]),
                  reads=[(xsrc[1], gt, c2) for c2 in range(2)], writes=[("xa", sl)], key=("xa", sl))
            if gt >= 1:
                flush()
            S.op("act", lambda e, sl=sl, col=col: e.activation(out=junk[:, :], in_=xa[sl][:, :], func=AF.Square,
                                                               accum_out=stat[:, col:col + 1]),
                 reads=[("xa", sl)], writes=[("wstg", 0), ("ss", col)], cost=1250.0)
            S.op("act:sqrt", lambda e, col=col: e.activation(out=stat[:, 8 + col:9 + col], in_=stat[:, col:col + 1], func=AF.Sqrt,
                                                        scale=1.0 / D, bias=EPS), reads=[("ss", col)], writes=[("sd", col)], cost=250.0)
            S.op("dve", lambda e, col=col: e.reciprocal(out=stat[:, 16 + col:17 + col], in_=stat[:, 8 + col:9 + col]),
                 reads=[("sd", col)], writes=[("rs", col)], cost=120.0)
            S.op("dve", lambda e, sl=sl, col=col: e.scalar_tensor_tensor(out=xa[sl][:, :], in0=xa[sl][:, :],
                                                                         scalar=stat[:, 16 + col:17 + col], in1=fg_bc[:, :],
                                                                         op0=ALU.mult, op1=ALU.mult),
                 reads=[("xa", sl), ("rs", col), ("gate_bc", s)], writes=[("xa", sl)], cost=1250.0)
            defer(lambda e, sl=sl, gt=gt: e.dma_start(out=y_out[s][gt * 128:(gt + 1) * 128, :], in_=xa[sl][:, :]),
                  [("xa", sl)], [("y", s, gt)], ("yst", sl))
            yield 1
        flush()

    def stream(s):
        for l in range(depth):
            xsrc = (x_in[s], ("xin", s)) if l == 0 else (xs[s][(l - 1) % 2], ("xs", s, (l - 1) % 2))
            xdst = (xs[s][l % 2], ("xs", s, l % 2))
            setup_mod(s, l)
            yield from pass1(s, l, xsrc)
            yield from pass2(s, l, xsrc, xdst)
        yield from pass3(s, (xs[s][(depth - 1) % 2], ("xs", s, (depth - 1) % 2)))

    gens = [stream(s) for s in range(nseq)]
    alive = [True] * nseq
    prog_ = [0.0] * nseq
    nchs = [seq_lens[s] // T for s in range(nseq)]

    in_p2 = [False] * nseq
    blocked = [False] * nseq

    def step(s):
        if blocked[s]:
            if any(in_p2[o] for o in range(nseq) if o != s):
                o = [o for o in range(nseq) if o != s and in_p2[o]][0]
                step(o)
                return
            blocked[s] = False
            in_p2[s] = True
        try:
            tok = next(gens[s])
        except StopIteration:
            alive[s] = False
            in_p2[s] = False
            return
        if tok == "p2_begin":
            if any(in_p2[o] for o in range(nseq) if o != s):
                blocked[s] = True
            else:
                in_p2[s] = True
        elif tok == "chunk_end":
            in_p2[s] = False
            prog_[s] = math.floor(prog_[s]) + 1.0
        else:
            prog_[s] = min(prog_[s] + float(tok), math.floor(prog_[s]) + 0.99)

    if nseq == 2 and os.environ.get("K_INTERLEAVE", "1") == "1":
        lead = float(nchs[0])
        while alive[0] and prog_[0] < lead:
            step(0)
        while alive[0] or alive[1]:
            if not alive[1]:
                step(0)
            elif not alive[0]:
                step(1)
            else:
                r0 = (prog_[0] - lead) / nchs[0]
                r1 = prog_[1] / nchs[1]
                step(0 if r0 <= r1 else 1)
    else:
        for s in range(nseq):
            while alive[s]:
                step(s)
    flush()
    if SIM:
        S._schedule()
        S.n_sems = S.max_val = 0
    else:
        S.emit()
    return nc, S


_CACHE = {}


def run(inputs, seq_lens, depth, n_cores, seq_of_core, debug=False):
    key = (tuple(seq_lens), depth, debug)
    if key not in _CACHE:
        _CACHE[key] = build(seq_lens, depth, debug)[0]
    nc = _CACHE[key]
    wfh = host_images(inputs, depth)
    sm = host_smalls(inputs, depth)
    fg = np.ascontiguousarray(np.broadcast_to(np.asarray(inputs["final_gain"], np.float32)[None, :], (128, D)))
    bada = np.ascontiguousarray(np.asarray(inputs["b_ada"], np.float32)[:depth].reshape(1, depth * 3 * D))
    wada = np.ascontiguousarray(np.asarray(inputs["w_ada"], np.float32)[:depth])
    in_maps = []
    for c in range(n_cores):
        m = {"wf": wfh, "w_ada": wada, "b_ada": bada, "smalls": sm, "fgain": fg}
        for s, (xn, cn, idx) in enumerate(seq_of_core(c)):
            m["x%d" % s] = np.ascontiguousarray(np.asarray(inputs[xn][idx], np.float32))
            m["c%d" % s] = np.ascontiguousarray(np.asarray(inputs[cn][idx], np.float32).reshape(KC, 128).T)
        in_maps.append(m)
    res = run_bass_kernel_spmd(nc, in_maps, core_ids=list(range(n_cores)))
    return res.results


def kernel(**inputs):
    inputs = {k: np.asarray(v) for k, v in inputs.items()}
    B, SP = inputs["x_prompt"].shape[0], inputs["x_prompt"].shape[1]
    BS, SS = inputs["x_sample"].shape[0], inputs["x_sample"].shape[1]
    depth = inputs["w_in"].shape[0]
    n = 8
    res = run(inputs, [SP, SS], depth, n,
              lambda c: [("x_prompt", "c_prompt", c % B), ("x_sample", "c_sample", c % BS)])
    yp = np.stack([res[c]["y0"] for c in range(B)], axis=0).astype(np.float32)
    ysm = np.stack([res[c]["y1"] for c in range(BS)], axis=0).astype(np.float32)
    return (yp, ysm)
```
